# Optimizing a Trainium2 kernel written in Bass

```python
import jax, jax.numpy as jnp
from jax import lax
import numpy as np

D_MODEL = 1024
BATCH = 8
SEQ = 2048
DEPTH = 4

CHUNK = 64
RMS_EPS = 1e-6
DN_HEADS = 4
DN_DK = 128
DN_DV = 128
DN_CONV = 4
DN_QKV = 2 * DN_HEADS * DN_DK + DN_HEADS * DN_DV
DN_VAL = DN_HEADS * DN_DV
RET_HEADS = 4
RET_DK = 128
RET_DV = 128
RET_QK = RET_HEADS * RET_DK
RET_VAL = RET_HEADS * RET_DV
ROPE_BASE = 10000.0
GN_EPS = 1e-5
POOL_WINDOWS = (2, 4, 8, 16)
POOL_GROUPS = 4
POOL_GROUP = 128
POOL_WIDTH = POOL_GROUPS * POOL_GROUP
N_BRANCH = 3
D_FF = -(-8 * D_MODEL // (3 * 256)) * 256
IN_SIZES = (DN_QKV, DN_VAL, DN_HEADS, DN_HEADS, RET_QK, RET_QK, RET_VAL, RET_VAL, POOL_WIDTH, N_BRANCH * D_MODEL)
D_IN = sum(IN_SIZES)

kernel_name = "hybrid_deltanet_pool_retention_block"


def _split_points(sizes):
    pts, acc = [], 0
    for s in sizes[:-1]:
        acc += s
        pts.append(acc)
    return pts


def rms_norm(x, w):
    xf = x.astype(jnp.float32)
    y = xf * lax.rsqrt(jnp.mean(xf * xf, axis=-1, keepdims=True) + RMS_EPS) * w.astype(jnp.float32)
    return y.astype(x.dtype)


def l2_normalize(x):
    return x * lax.rsqrt(jnp.sum(x * x, axis=-1, keepdims=True) + 1e-6)


def causal_depthwise_conv(x, w):
    k = w.shape[0]
    return lax.conv_general_dilated(
        x, w[:, None, :].astype(x.dtype), window_strides=(1,), padding=((k - 1, 0),),
        dimension_numbers=("NWC", "WIO", "NWC"), feature_group_count=x.shape[-1])


def to_chunks(x):
    b, s, h, d = x.shape
    return x.reshape(b, s // CHUNK, CHUNK, h, d).transpose(0, 3, 1, 2, 4)


def from_chunks(x):
    b, h, n, c, d = x.shape
    return x.transpose(0, 2, 3, 1, 4).reshape(b, n * c, h * d)


def gated_delta_rule(q, k, v, beta, g):
    b, s, h, dk = q.shape
    dv = v.shape[-1]
    n = s // CHUNK
    q = to_chunks(q) * (dk ** -0.5)
    k = to_chunks(k)
    v = to_chunks(v)
    beta = beta.reshape(b, n, CHUNK, h).transpose(0, 3, 1, 2)
    g = g.reshape(b, n, CHUNK, h).transpose(0, 3, 1, 2)
    G = jnp.cumsum(g, axis=-1)
    causal = jnp.tril(jnp.ones((CHUNK, CHUNK), dtype=bool))
    strict = jnp.tril(jnp.ones((CHUNK, CHUNK), dtype=bool), k=-1)
    diff = G[..., :, None] - G[..., None, :]
    decay = jnp.where(causal, jnp.exp(jnp.where(causal, diff, 0.0)), 0.0)
    k_beta = k * beta[..., None]
    m = jnp.where(strict, jnp.einsum('bhnid,bhnjd->bhnij', k_beta, k) * decay, 0.0)
    a = m + jnp.eye(CHUNK, dtype=m.dtype)
    rhs = jnp.concatenate([v * beta[..., None], k_beta * jnp.exp(G)[..., None]], axis=-1)
    sol = lax.linalg.triangular_solve(a, rhs, left_side=True, lower=True, unit_diagonal=True)
    u, w = sol[..., :dv], sol[..., dv:]
    attn = jnp.einsum('bhnid,bhnjd->bhnij', q, k) * decay
    g_last = G[..., -1]
    q_dec = q * jnp.exp(G)[..., None]
    k_tail = k * jnp.exp(g_last[..., None] - G)[..., None]
    xs = tuple(jnp.moveaxis(t, 2, 0) for t in (u, w, attn, q_dec, k_tail, jnp.exp(g_last)))

    def step(state, inp):
        u_c, w_c, a_c, qd_c, kt_c, dl_c = inp
        v_new = u_c - jnp.einsum('bhck,bhkv->bhcv', w_c, state)
        o = jnp.einsum('bhck,bhkv->bhcv', qd_c, state) + jnp.einsum('bhcj,bhjv->bhcv', a_c, v_new)
        state = state * dl_c[..., None, None] + jnp.einsum('bhck,bhcv->bhkv', kt_c, v_new)
        return state, o

    s0 = jnp.zeros((b, h, dk, dv), dtype=q.dtype)
    _, o = lax.scan(step, s0, xs)
    return from_chunks(jnp.moveaxis(o, 0, 2))


def rotary(x):
    s, d = x.shape[1], x.shape[-1]
    half = d // 2
    inv = ROPE_BASE ** (-jnp.arange(half, dtype=jnp.float32) / half)
    ang = jnp.arange(s, dtype=jnp.float32)[:, None] * inv[None, :]
    cos = jnp.cos(ang)[None, :, None, :]
    sin = jnp.sin(ang)[None, :, None, :]
    x1, x2 = x[..., :half], x[..., half:]
    return jnp.concatenate([x1 * cos - x2 * sin, x2 * cos + x1 * sin], axis=-1)


def chunkwise_retention(q, k, v, log_gamma):
    b, s, h, dk = q.shape
    dv = v.shape[-1]
    q = to_chunks(q)
    k = to_chunks(k) * (dk ** -0.5)
    v = to_chunks(v)
    idx = jnp.arange(CHUNK, dtype=jnp.float32)
    lg = log_gamma[:, None]
    dmask = jnp.exp(jnp.abs(idx[:, None] - idx[None, :])[None] * log_gamma[:, None, None])
    scores = jnp.einsum('bhnid,bhnjd->bhnij', q, k) * dmask[:, None]
    o_inner = jnp.einsum('bhnij,bhnjv->bhniv', scores, v)
    xi = jnp.exp((idx + 1.0)[None] * lg)
    zeta = jnp.exp((CHUNK - 1.0 - idx)[None] * lg)
    kv = jnp.einsum('bhnjd,bhnjv->bhndv', k * zeta[:, None, :, None], v)
    chunk_decay = jnp.exp(CHUNK * log_gamma)

    def step(r, kv_c):
        return r * chunk_decay[:, None, None] + kv_c, r

    _, r_prev = lax.scan(step, jnp.zeros((b, h, dk, dv), dtype=kv.dtype), jnp.moveaxis(kv, 2, 0))
    r_prev = jnp.moveaxis(r_prev, 0, 2)
    o_cross = jnp.einsum('bhnid,bhndv->bhniv', q * xi[:, None, :, None], r_prev)
    o = from_chunks(o_inner + o_cross)
    return o.reshape(b, s, h, dv)


def multiscale_pool(u, w_lin, scale):
    b, s, _ = u.shape
    groups = u.reshape(b, s, POOL_GROUPS, POOL_GROUP)
    cs = jnp.cumsum(groups, axis=1)
    t = jnp.arange(1, s + 1, dtype=jnp.float32)
    pooled = []
    for gi, win in enumerate(POOL_WINDOWS):
        c = cs[:, :, gi]
        prev = jnp.pad(c, ((0, 0), (win, 0), (0, 0)))[:, :s]
        cnt = jnp.minimum(t, float(win))[None, :, None]
        pooled.append((c - prev) / cnt)
    mixed = jnp.stack(pooled, axis=2) - groups
    y = jnp.einsum('bsgc,gcd->bsgd', mixed, w_lin.astype(jnp.float32)).reshape(b, s, POOL_WIDTH)
    return y * scale.astype(jnp.float32)


def hybrid_mixer(h, w_in, dn_conv, dn_A_log, dn_dt_bias, dn_out_norm, ret_out_norm,
                 pool_w, pool_scale, w_branch_dn, w_branch_ret, w_branch_pool, w_out):
    b, s, _ = h.shape
    f32 = jnp.float32
    proj = h @ w_in
    (dn_qkv, dn_z, dn_b, dn_a, ret_q, ret_k, ret_v, ret_g, pool_u, gates) = jnp.split(
        proj, _split_points(IN_SIZES), axis=-1)

    qkv = jax.nn.silu(causal_depthwise_conv(dn_qkv, dn_conv)).astype(f32)
    q, k, v = jnp.split(qkv, [DN_HEADS * DN_DK, 2 * DN_HEADS * DN_DK], axis=-1)
    q = l2_normalize(q.reshape(b, s, DN_HEADS, DN_DK))
    k = l2_normalize(k.reshape(b, s, DN_HEADS, DN_DK))
    v = v.reshape(b, s, DN_HEADS, DN_DV)
    beta = jax.nn.sigmoid(dn_b.astype(f32))
    g = -jnp.exp(dn_A_log.astype(f32)) * jax.nn.softplus(dn_a.astype(f32) + dn_dt_bias.astype(f32))
    o_dn = gated_delta_rule(q, k, v, beta, g).reshape(b, s, DN_HEADS, DN_DV)
    o_dn = o_dn * lax.rsqrt(jnp.mean(o_dn * o_dn, axis=-1, keepdims=True) + RMS_EPS) * dn_out_norm.astype(f32)
    o_dn = o_dn.reshape(b, s, DN_VAL) * jax.nn.silu(dn_z.astype(f32))

    o_pool = multiscale_pool(pool_u.astype(f32), pool_w, pool_scale)

    log_gamma = jnp.log1p(-jnp.exp2(-5.0 - jnp.arange(RET_HEADS, dtype=f32)))
    rq = rotary(ret_q.astype(f32).reshape(b, s, RET_HEADS, RET_DK))
    rk = rotary(ret_k.astype(f32).reshape(b, s, RET_HEADS, RET_DK))
    rv = ret_v.astype(f32).reshape(b, s, RET_HEADS, RET_DV)
    o_ret = chunkwise_retention(rq, rk, rv, log_gamma)
    mu = jnp.mean(o_ret, axis=-1, keepdims=True)
    var = jnp.mean(jnp.square(o_ret - mu), axis=-1, keepdims=True)
    o_ret = ((o_ret - mu) * lax.rsqrt(var + GN_EPS)).reshape(b, s, RET_VAL) * ret_out_norm.astype(f32)
    o_ret = o_ret * jax.nn.silu(ret_g.astype(f32))

    gt = jax.nn.sigmoid(gates.astype(f32)).reshape(b, s, N_BRANCH, D_MODEL)
    y = (gt[:, :, 0] * (o_dn.astype(h.dtype) @ w_branch_dn)
         + gt[:, :, 1] * (o_pool.astype(h.dtype) @ w_branch_pool)
         + gt[:, :, 2] * (o_ret.astype(h.dtype) @ w_branch_ret))
    return y.astype(h.dtype) @ w_out


def swiglu(h, w_gate, w_up, w_down):
    return (jax.nn.silu(h @ w_gate) * (h @ w_up)) @ w_down


def setup_inputs(seed: int = 0) -> dict:
    key = jax.random.key(seed)
    ks = jax.random.split(key, 24)
    f32 = jnp.float32
    L = DEPTH

    def nrm(k, shape, fan_in):
        return jax.random.normal(k, shape, f32) * (fan_in ** -0.5)

    def gain(k, shape):
        return 1.0 + 0.1 * jax.random.normal(k, shape, f32)

    dt = jnp.exp(jax.random.uniform(ks[6], (L, DN_HEADS), f32, minval=float(np.log(1e-3)), maxval=float(np.log(1e-1))))
    return {
        "x": jax.random.normal(ks[0], (BATCH, SEQ, D_MODEL), f32),
        "mix_pre_norm": gain(ks[1], (L, D_MODEL)),
        "mix_post_norm": gain(ks[2], (L, D_MODEL)),
        "w_in": nrm(ks[3], (L, D_MODEL, D_IN), D_MODEL),
        "dn_conv": nrm(ks[4], (L, DN_CONV, DN_QKV), DN_CONV),
        "dn_A_log": jnp.log(jax.random.uniform(ks[5], (L, DN_HEADS), f32, minval=1.0, maxval=16.0)),
        "dn_dt_bias": dt + jnp.log(-jnp.expm1(-dt)),
        "dn_out_norm": gain(ks[7], (L, DN_DV)),
        "ret_out_norm": gain(ks[8], (L, RET_VAL)),
        "pool_w": nrm(ks[9], (L, POOL_GROUPS, POOL_GROUP, POOL_GROUP), POOL_GROUP),
        "pool_scale": gain(ks[10], (L, POOL_WIDTH)),
        "w_branch_dn": nrm(ks[11], (L, DN_VAL, D_MODEL), DN_VAL),
        "w_branch_ret": nrm(ks[12], (L, RET_VAL, D_MODEL), RET_VAL),
        "w_branch_pool": nrm(ks[13], (L, POOL_WIDTH, D_MODEL), POOL_WIDTH),
        "w_out": nrm(ks[14], (L, D_MODEL, D_MODEL), D_MODEL),
        "ffn_pre_norm": gain(ks[15], (L, D_MODEL)),
        "ffn_post_norm": gain(ks[16], (L, D_MODEL)),
        "ffn_gate": nrm(ks[17], (L, D_MODEL, D_FF), D_MODEL),
        "ffn_up": nrm(ks[18], (L, D_MODEL, D_FF), D_MODEL),
        "ffn_down": nrm(ks[19], (L, D_FF, D_MODEL), D_FF),
    }


def reference(x, mix_pre_norm, mix_post_norm, w_in, dn_conv, dn_A_log, dn_dt_bias, dn_out_norm,
              ret_out_norm, pool_w, pool_scale, w_branch_dn, w_branch_ret, w_branch_pool, w_out,
              ffn_pre_norm, ffn_post_norm, ffn_gate, ffn_up, ffn_down):
    for l in range(DEPTH):
        h = rms_norm(x, mix_pre_norm[l])
        m = hybrid_mixer(h, w_in[l], dn_conv[l], dn_A_log[l], dn_dt_bias[l], dn_out_norm[l],
                         ret_out_norm[l], pool_w[l], pool_scale[l], w_branch_dn[l], w_branch_ret[l],
                         w_branch_pool[l], w_out[l])
        x = x + rms_norm(m, mix_post_norm[l])
        h = rms_norm(x, ffn_pre_norm[l])
        x = x + rms_norm(swiglu(h, ffn_gate[l], ffn_up[l], ffn_down[l]), ffn_post_norm[l])
    return x
```

```python
import contextlib
import numpy as np
import ml_dtypes
import concourse.bass as bass
import concourse.mybir as mybir
from concourse.bass_utils import run_bass_kernel_spmd

F32 = mybir.dt.float32
BF16 = mybir.dt.bfloat16
AF = mybir.ActivationFunctionType
ALU = mybir.AluOpType
AX = mybir.AxisListType

SEQ = 2048
D = 1024
NT = 16
DFF = 2816
NJ = 22
D_IN = 7688
C_Q, C_K, C_V, C_Z, C_BA = 0, 512, 1024, 1536, 2048
C_RQ, C_RK, C_RV, C_RG, C_PU, C_GT = 2056, 2568, 3080, 3592, 4104, 4616
NSM = 228
NEG = -30000.0


class Sched:
    def __init__(self, nc, n_dma_slots=6):
        self.nc = nc
        self.engs = {"pe": nc.tensor, "dve": nc.vector, "act": nc.scalar, "pool": nc.gpsimd, "sp": nc.sync}
        self.sem = {k: nc.alloc_semaphore(name="c_" + k) for k in self.engs}
        self.cnt = {k: 0 for k in self.engs}
        self.seen = {k: {} for k in self.engs}
        self.dma_sems, self.dma_cnt, self.dma_pos = {}, {}, {}
        self.n_dma_slots = n_dma_slots
        self.lastw, self.readers = {}, {}
        self.off = False

    def _wait(self, e, tok):
        sem, val, src = tok
        if src == e and e == "pe":
            return
        if src == e and self.cnt[e] - val >= 3:
            return
        key = id(sem)
        if self.seen[e].get(key, 0) >= val:
            return
        self.engs[e].wait_ge(sem, val)
        self.seen[e][key] = val

    def _deps(self, e, reads, writes):
        for k in reads:
            t = self.lastw.get(k)
            if t is not None:
                self._wait(e, t)
        for k in writes:
            t = self.lastw.get(k)
            if t is not None:
                self._wait(e, t)
            for t in self.readers.get(k, ()):
                self._wait(e, t)

    def _record(self, tok, reads, writes):
        for k in writes:
            self.lastw[k] = tok
            self.readers[k] = []
        for k in reads:
            self.readers.setdefault(k, []).append(tok)

    @staticmethod
    def _norm(reads, writes):
        r2, w2 = [], []
        for k in reads:
            if k[:2] in ("PA", "PB") and len(k) >= 3 and k[2].isdigit():
                w2.append(k[:3])
            else:
                r2.append(k)
        for k in writes:
            if k[:2] in ("PA", "PB") and len(k) >= 3 and k[2].isdigit():
                w2.append(k[:3])
            else:
                w2.append(k)
        return r2, w2

    def op(self, e, fn, reads=(), writes=()):
        if self.off:
            return
        reads, writes = self._norm(reads, writes)
        self._deps(e, reads, writes)
        ins = fn(self.engs[e])
        self.cnt[e] += 1
        ins.then_inc(self.sem[e], 1)
        self._record((self.sem[e], self.cnt[e], e), reads, writes)

    def dma(self, q, out, in_, reads=(), writes=()):
        if self.off:
            return
        if q not in self.dma_sems:
            self.dma_sems[q] = [self.nc.alloc_semaphore(name=f"d_{q}_{i}") for i in range(self.n_dma_slots)]
            self.dma_cnt[q] = [0] * self.n_dma_slots
            self.dma_pos[q] = 0
        s = self.dma_pos[q]
        self.dma_pos[q] = (s + 1) % self.n_dma_slots
        sem = self.dma_sems[q][s]
        if self.dma_cnt[q][s] > 0:
            self._wait(q, (sem, 16 * self.dma_cnt[q][s], "dma"))
        self._deps(q, reads, writes)
        ins = self.engs[q].dma_start(out=out, in_=in_)
        self.dma_cnt[q][s] += 1
        ins.then_inc(sem, 16)
        self._record((sem, 16 * self.dma_cnt[q][s], "dma"), reads, writes)

    def barrier(self):
        if self.off:
            return
        toks = [(self.sem[e], self.cnt[e], e) for e in self.engs if self.cnt[e] > 0]
        for q in self.dma_sems:
            for s, sem in enumerate(self.dma_sems[q]):
                if self.dma_cnt[q][s] > 0:
                    toks.append((sem, 16 * self.dma_cnt[q][s], "dma"))
        for e in self.engs:
            for t in toks:
                if t[2] != e:
                    self._wait(e, t)
        self.lastw, self.readers = {}, {}


def host_consts():
    c = {}
    i = np.arange(128)
    c["c_identf"] = np.eye(128, dtype=np.float32)
    c["c_onesf"] = np.ones((128, 128), np.float32)
    c["c_ltri"] = (i[:, None] <= i[None, :]).astype(np.float32)
    nm = np.zeros((128, 4, 128), np.float32)
    nm[:, 0, :] = np.where(i[:, None] > i[None, :], 0.0, NEG)
    nm[:, 1, :] = np.where(i[None, :] > i[:, None], 0.0, NEG)
    nm[:, 2, :] = np.where(i[None, :] >= i[:, None], 0.0, NEG)
    nm[:, 3, :] = np.where(i[None, :] >= i[:, None], 0.0, -NEG)
    c["c_nm"] = nm
    lg = np.log1p(-np.exp2(-5.0 - np.arange(4))).astype(np.float64)
    same = (i[:, None] // 64) == (i[None, :] // 64)
    lower = (i[:, None] // 64) > (i[None, :] // 64)
    rmask = np.zeros((128, 4, 128), np.float64)
    xi = np.zeros((128, 4, 128), np.float64)
    zeta = np.zeros((128, 4, 128), np.float64)
    for h in range(4):
        gam = np.exp(lg[h])
        M = np.where(same, gam ** np.abs(i[:, None] - i[None, :]),
                     np.where(lower, gam ** np.clip(i[:, None] - i[None, :], 0, None), 0.0)) * 128 ** -0.5
        rmask[:, h, :] = M.T
        xi[:, h, :] = (gam ** (i + 1.0))[None, :]
        zeta[:, h, :] = (gam ** (127.0 - i) * 128 ** -0.5)[:, None]
    c["c_rmask"] = rmask.astype(np.float32)
    c["c_xi"] = xi.astype(np.float32)
    c["c_zeta"] = zeta.astype(np.float32)
    c["_rcd"] = [float(np.exp(lg[h]) ** 128) for h in range(4)]
    half = 64
    inv = 10000.0 ** (-np.arange(half, dtype=np.float32) / half)
    ang = np.arange(SEQ, dtype=np.float32)[:, None] * inv[None, :]
    cos = np.cos(ang).astype(np.float32).reshape(NT, 128, 64).transpose(1, 0, 2)
    sin = np.sin(ang).astype(np.float32).reshape(NT, 128, 64).transpose(1, 0, 2)
    c["c_cos"] = np.ascontiguousarray(cos)
    c["c_sin"] = np.ascontiguousarray(sin)
    l0 = np.zeros((128, 3, 4, NT, 2), np.float32)
    l0[0, 0, :, :, 1] = 1.0
    l0[0, 2, :, :, 0] = 1.0
    c["c_l0"] = l0
    pinv = np.zeros((128, 4, 16), np.float32)
    for g, w in enumerate((2, 4, 8, 16)):
        pinv[:, g, :] = 1.0 / np.minimum(np.arange(1, 17), w)
    c["c_pinv"] = pinv
    return c


CONST_SHAPES = {
    "c_identf": [128, 128], "c_onesf": [128, 128], "c_ltri": [128, 128], "c_nm": [128, 4, 128],
    "c_rmask": [128, 4, 128], "c_xi": [128, 4, 128], "c_zeta": [128, 4, 128],
    "c_cos": [128, NT, 64], "c_sin": [128, NT, 64], "c_l0": [128, 3, 4, NT, 2], "c_pinv": [128, 4, 16],
}
W_SHAPES = {
    "w_in": [D, D_IN], "pool_w": [4, 128, 128], "w_branch_dn": [512, D], "w_branch_ret": [512, D],
    "w_branch_pool": [512, D], "w_out": [D, D], "ffn_gate": [D, DFF], "ffn_up": [D, DFF], "ffn_down": [DFF, D],
}


class _Stop(Exception):
    pass


STOP = [99]


def build_program(NL, rcd, dbg=False):
    nc = bass.Bass("TRN2", target_bir_lowering=False)
    S = Sched(nc)
    xT = nc.dram_tensor("xT", [D, SEQ], F32, kind="ExternalInput").ap()
    yT = nc.dram_tensor("yT", [D, SEQ], F32, kind="ExternalOutput").ap()
    psmall = nc.dram_tensor("p_small", [NL, 128, NSM], F32, kind="ExternalInput").ap()
    cd = {k: nc.dram_tensor(k, shp, F32, kind="ExternalInput").ap() for k, shp in CONST_SHAPES.items()}
    wd = {k: nc.dram_tensor(k, [NL] + shp, F32, kind="ExternalInput").ap() for k, shp in W_SHAPES.items()}
    dbg_outs = {}

    def mm(out, lhsT, rhs, start, stop, r, w):
        S.op("pe", lambda e: e.matmul(out, lhsT, rhs, start=start, stop=stop), reads=r, writes=w)

    def tr(out, in_, ident, r, w):
        S.op("pe", lambda e: e.transpose(out, in_, ident), reads=r, writes=w)

    def act(out, in_, func, r, w, bias=None, scale=None):
        kw = {}
        if bias is not None:
            kw["bias"] = bias
        if scale is not None:
            kw["scale"] = scale
        S.op("act", lambda e: e.activation(out, in_, func, **kw), reads=r, writes=w)

    def tt(eng, out, in0, in1, op, r, w):
        S.op(eng, lambda e: e.tensor_tensor(out, in0, in1, op), reads=r, writes=w)

    def ts(eng, out, in0, s1, s2, op0, op1, r, w):
        if s2 is None:
            S.op(eng, lambda e: e.tensor_scalar(out, in0, s1, None, op0), reads=r, writes=w)
        else:
            S.op(eng, lambda e: e.tensor_scalar(out, in0, s1, s2, op0, op1), reads=r, writes=w)

    def stt(out, in0, sc, in1, op0, op1, r, w):
        S.op("dve", lambda e: e.scalar_tensor_tensor(out, in0, sc, in1, op0, op1), reads=r, writes=w)

    def cp(eng, out, in_, r, w):
        if eng == "act":
            S.op("act", lambda e: e.copy(out, in_), reads=r, writes=w)
        else:
            S.op(eng, lambda e: e.tensor_copy(out, in_), reads=r, writes=w)

    def bks(name, c0, c1):
        return [f"{name}{b}" for b in range(c0 // 512, (c1 - 1) // 512 + 1)]

    with contextlib.ExitStack() as g:
        tcount = [0]

        def T(st, name, shape, dt=F32):
            tcount[0] += 1
            return st.enter_context(nc.sbuf_tensor(f"{name}_{tcount[0]}", shape, dt))

        PA = g.enter_context(nc.psum_tensor("PA", [128, 2048], F32))
        PB = g.enter_context(nc.psum_tensor("PB", [128, 2048], F32))
        PAb = PA[:].bitcast(BF16)
        PBb = PB[:].bitcast(BF16)
        X = T(g, "X", [128, 8, SEQ])
        RET_C = ("c_rmask", "c_xi", "c_zeta", "c_cos", "c_sin")
        cs = {k: T(g, "s" + k, shp) for k, shp in CONST_SHAPES.items() if k != "c_l0" and k not in RET_C}
        identb = T(g, "identb", [128, 128], BF16)
        onesb = T(g, "onesb", [128, 128], BF16)
        nmb = T(g, "nmb", [128, 4, 128], BF16)
        ltrib = T(g, "ltrib", [128, 128], BF16)
        LABC = T(g, "LABC", [128, 3, 4, NT, 2])
        PS = T(g, "PSm", [128, NSM])
        Rst = T(g, "Rst", [128, 4, 128])
        Rb = T(g, "Rb", [128, 4, 128], BF16)
        Sst = T(g, "Sst", [128, 128])
        Sb = T(g, "Sb", [128, 128], BF16)

        for k in cs:
            S.dma("sp", cs[k][:], cd[k], writes=[k])
        S.dma("sp", LABC[:], cd["c_l0"], writes=["LABC"])
        for f in range(8):
            S.dma("sp", X[:, f, :], xT[f * 128:(f + 1) * 128, :], writes=[f"X{f}"])
        cp("dve", identb[:], cs["c_identf"][:], ["c_identf"], ["identb"])
        cp("dve", onesb[:], cs["c_onesf"][:], ["c_onesf"], ["onesb"])
        cp("dve", nmb[:], cs["c_nm"][:], ["c_nm"], ["nmb"])
        cp("dve", ltrib[:], cs["c_ltri"][:], ["c_ltri"], ["ltrib"])
        identf, onesf, ltri = cs["c_identf"], cs["c_onesf"], cs["c_ltri"]

        def rsqrt(out, in_, scale, eps, n, r, w, shape3=None):
            act(out, in_, AF.Ln, r, w, bias=eps, scale=scale)
            act(out, out, AF.Exp, list(w), w, scale=-0.5)

        def dump(name, ap, shape, dt, keys):
            if not dbg:
                return
            o = nc.dram_tensor("dbg_" + name, shape, dt, kind="ExternalOutput").ap()
            S.dma("sp", o, ap, reads=keys)
            dbg_outs[name] = (shape, dt)

        def ckpt(stage):
            if STOP[0] == stage and not S.off:
                S.barrier()
                S.off = True

        wcount = [0]

        import os
        skipw = os.environ.get("SKIPW", "")

        def wload(dst, src, key):
            if skipw and key[:3] in skipw.split(","):
                return
            S.dma("pool", dst, src, writes=[key])

        def win_cols(l, c0, w):
            return wd["w_in"][l].rearrange("(k p) n -> p k n", p=128)[:, :, c0:c0 + w]

        def rmsnorm(src_fn, src_keys, wcol0, dst_fn, dst_keys, blks, st):
            SQ = T(st, "n_SQ", [128, 2, 512], BF16)
            RS = T(st, "n_RS", [128, 512])
            for bi, blk in enumerate(blks):
                for f in range(8):
                    act(SQ[:, f % 2, :], src_fn(f, blk), AF.Square, [src_keys(f)], [f"nSQ{f % 2}"])
                    mm(PB[:, 1536:2048], onesb[:], SQ[:, f % 2, :], f == 0, f == 7, ["onesb", f"nSQ{f % 2}"], ["PB3"])
                rsqrt(RS[:], PB[:, 1536:2048], 1.0 / D, 1e-6, 512, ["PB3"], ["nRS"])
                for f in range(8):
                    stt(dst_fn(f, bi), src_fn(f, blk), PS[:, wcol0 + f:wcol0 + f + 1], RS[:], ALU.mult, ALU.mult,
                        [src_keys(f), "nRS", "PSm"], [dst_keys(f)])

        def postnorm_residual(M, blk, wcol0, st_keys):
            pass

        try:
          for l in range(NL):
            S.dma("sp", PS[:], psmall[l], writes=["PSm"])
            with contextlib.ExitStack() as mx:
                HT = T(mx, "HT", [128, 8, SEQ], BF16)
                OB = T(mx, "OB", [128, 4, SEQ], BF16)
                with contextlib.ExitStack() as st:
                    rmsnorm(lambda f, b: X[:, f, b * 512:(b + 1) * 512], lambda f: f"X{f}", 0,
                            lambda f, bi: HT[:, f, bi * 512:(bi + 1) * 512], lambda f: f"H{f}", range(4), st)
                S.barrier()
                if l == 0:
                    dump("HT", HT[:], [128, 8, SEQ], BF16, [f"H{f}" for f in range(8)])
                ckpt(0)

                def proj_fm(Wt, wkey, dst, dkey):
                    for kc in range(8):
                        for b in range(4):
                            mm(dst[:, b * 512:(b + 1) * 512], Wt[:, kc, :], HT[:, kc, b * 512:(b + 1) * 512],
                               kc == 0, kc == 7, [wkey, f"H{kc}"], [f"{dkey}{b}"])

                with contextlib.ExitStack() as st:
                    WD = T(st, "WD", [128, 2, 8, 128], BF16)
                    WBA = T(st, "WBA", [128, 8, 8], BF16)
                    QT = T(st, "QT", [128, SEQ], BF16)
                    KT = T(st, "KT", [128, SEQ], BF16)
                    VT = T(st, "VT", [128, SEQ], BF16)
                    XS = T(st, "XS", [128, 4 + SEQ], BF16)
                    SQ = T(st, "SQ", [128, 512], BF16)
                    RN = T(st, "RN", [128, 512])
                    DG = T(st, "DG", [128, 4, 128], BF16)
                    K_TM = T(st, "K_TM", [128, NT, 128], BF16)
                    V_TM = T(st, "V_TM", [128, NT, 128], BF16)
                    KBG = T(st, "KBG", [128, NT, 128], BF16)
                    U = T(st, "U", [128, 2, 4, 128])
                    WT = T(st, "WT", [128, 2, 4, 128], BF16)
                    ATT = T(st, "ATT", [128, 2, 4, 128], BF16)
                    ONB = T(st, "ONB", [128, 4, 128], BF16)
                    CH = T(st, "CH", [128, 2, 3, 4, 128])
                    TTb = T(st, "TTb", [128, 4, 128], BF16)
                    ES = T(st, "ES", [128, 2, 512])
                    NGL = T(st, "NGL", [128, 2, 2, 128], BF16)
                    GHL = T(st, "GHL", [128, 2, NT, 4], BF16)
                    GBt = T(st, "GBt", [128, NT, 4])
                    NGt = T(st, "NGt", [128, NT, 4])
                    BA = T(st, "BA", [128, NT, 8])
                    BETA = T(st, "BETA", [128, NT, 4])
                    LBt = T(st, "LBt", [128, NT, 4])
                    G = T(st, "G", [128, NT, 4])
                    TMP = T(st, "TMP", [128, NT, 4])
                    GTs = T(st, "GTs", [128, NT, 4])
                    EG = T(st, "EG", [128, NT, 4])
                    DL = T(st, "DL", [128, NT, 4])
                    ETL = T(st, "ETL", [128, NT, 4])
                    BEG = T(st, "BEG", [128, NT, 4])
                    RQ = T(st, "RQ", [128, NT])
                    SC1 = T(st, "SC1", [128, NT])
                    SC2 = T(st, "SC2", [128, NT])
                    VN = T(st, "VN", [128, 128], BF16)
                    BS = T(st, "BS", [128, 128])
                    OT = T(st, "OT", [128, 128])
                    OSQ = T(st, "OSQ", [128, 128])
                    SS = T(st, "SS", [128, 2])
                    SZ = XS[:, 4:4 + SEQ]

                    S.op("dve", lambda e: e.memset(XS[:, 0:4], 0.0), writes=["XS"])
                    wload(WD[:, 0], win_cols(l, C_BA - 120, 128), "WD0")
                    for t in range(NT):
                        for kc in range(8):
                            mm(PA[:, t * 8:(t + 1) * 8], HT[:, kc, t * 128:(t + 1) * 128], WD[:, 0, kc, 120:128], kc == 0, kc == 7,
                               ["WD0", f"H{kc}"], ["PA0"])
                    ckpt(10)
                    cp("act", BA[:].rearrange("p t c -> p (t c)"), PA[:, 0:128], ["PA0"], ["BA"])
                    act(BETA[:], BA[:, :, 0:4], AF.Sigmoid, ["BA"], ["BETA"])
                    act(LBt[:], BETA[:], AF.Ln, ["BETA"], ["LBt"])
                    dtb = PS[:, 164:228].rearrange("p (t h) -> p t h", h=4)
                    alog = PS[:, 100:164].rearrange("p (t h) -> p t h", h=4)
                    tt("dve", TMP[:], BA[:, :, 4:8], dtb, ALU.add, ["BA", "PSm"], ["TMP"])
                    act(TMP[:], TMP[:], AF.Exp, ["TMP"], ["TMP"])
                    act(TMP[:], TMP[:], AF.Ln, ["TMP"], ["TMP"], bias=1.0)
                    act(G[:], alog, AF.Exp, ["PSm"], ["G"])
                    stt(G[:], TMP[:], -1.0, G[:], ALU.mult, ALU.mult, ["TMP", "G"], ["G"])
                    ckpt(11)
                    cp("dve", LABC[:, 0, :, :, 0], G[:].rearrange("p t h -> p h t"), ["G"], ["LABC"])
                    cp("dve", LABC[:, 1, :, :, 0], LBt[:].rearrange("p t h -> p h t"), ["LBt"], ["LABC"])
                    ts("dve", LABC[:, 2, :, :, 1], G[:].rearrange("p t h -> p h t"), -1.0, None, ALU.mult, None, ["G"], ["LABC"])
                    ckpt(12)
                    for t in range(NT):
                        mm(PA[:, 512 + t * 4:512 + (t + 1) * 4], ltri[:], G[:, t, :], True, True, ["c_ltri", "G"], ["PA1"])
                    for t in range(NT):
                        mm(PA[:, 1024 + t * 4:1024 + (t + 1) * 4], onesf[:], G[:, t, :], True, True, ["c_onesf", "G"], ["PA2"])
                    ckpt(13)
                    flat = lambda a: a[:].rearrange("p t h -> p (t h)")
                    cp("dve", flat(GTs), PA[:, 512:576], ["PA1"], ["GTs"])
                    ckpt(14)
                    act(flat(EG), PA[:, 512:576], AF.Exp, ["PA1"], ["EG"])
                    ckpt(15)
                    act(flat(DL), PA[:, 1024:1088], AF.Exp, ["PA2"], ["DL"])
                    tt("dve", flat(ETL), PA[:, 1024:1088], flat(GTs), ALU.subtract, ["PA2", "GTs"], ["ETL"])
                    act(flat(ETL), flat(ETL), AF.Exp, ["ETL"], ["ETL"])
                    ckpt(16)
                    tt("dve", BEG[:], BETA[:], EG[:], ALU.mult, ["BETA", "EG"], ["BEG"])
                    tt("dve", GBt[:], GTs[:], LBt[:], ALU.add, ["GTs", "LBt"], ["GBt"])
                    ts("dve", NGt[:], GTs[:], -1.0, None, ALU.mult, None, ["GTs"], ["NGt"])
                    ts("dve", GHL[:, 0], G[:], -1.0, None, ALU.mult, None, ["G"], ["GHL0"])
                    tt("dve", TMP[:], G[:], GHL[:, 0], ALU.add, ["G", "GHL0"], ["TMP"])
                    ts("dve", GHL[:, 1], TMP[:], -1.0, None, ALU.mult, None, ["TMP"], ["GHL1"])
                    ckpt(17)
                    if dbg and l == 0:
                        dump("G", G[:], [128, NT, 4], F32, ["G"])
                        dump("EG", EG[:], [128, NT, 4], F32, ["EG"])
                    ckpt(1)

                    for h in range(4):
                        for sec, dst, dk_ in ((0, QT, "QT"), (1, KT, "KT"), (2, VT, "VT")):
                            ti = sec * 4 + h
                            wi = wcount[0] % 2
                            wcount[0] += 1
                            wload(WD[:, wi], win_cols(l, sec * 512 + h * 128, 128), f"WD{wi}")
                            proj_fm(WD[:, wi], f"WD{wi}", PA, "PA")
                            cp("act", XS[:, 4:4 + SEQ], PA[:], bks("PA", 0, 2048), ["XS"])
                            for k in range(4):
                                ts("dve", DG[:, k, :], identb[:], PS[:, 32 + ti * 4 + k:33 + ti * 4 + k], None, ALU.mult, None,
                                   ["identb", "PSm"], [f"DG{k}"])
                            for b in range(4):
                                for k in range(4):
                                    mm(PB[:, b * 512:(b + 1) * 512], DG[:, k, :], XS[:, 1 + b * 512 + k:1 + b * 512 + k + 512],
                                       k == 0, k == 3, [f"DG{k}", "XS"], [f"PB{b}"])
                            act(dst[:], PB[:], AF.Silu, bks("PB", 0, 2048), [dk_])
                        if dbg and l == 0 and h == 0:
                            dump("QT0", QT[:], [128, SEQ], BF16, ["QT"])
                        ckpt(2)
                        wi = wcount[0] % 2
                        wcount[0] += 1
                        wload(WD[:, wi], win_cols(l, C_Z + h * 128, 128), f"WD{wi}")
                        proj_fm(WD[:, wi], f"WD{wi}", PB, "PB")
                        act(SZ, PB[:], AF.Silu, bks("PB", 0, 2048), ["XS"])
                        for b in range(4):
                            sl = slice(b * 512, (b + 1) * 512)
                            act(SQ[:], KT[:, sl], AF.Square, ["KT"], ["SQ"])
                            mm(PA[:, sl], onesb[:], SQ[:], True, True, ["onesb", "SQ"], [f"PA{b}"])
                            rsqrt(RN[:], PA[:, sl], 1.0, 1e-6, 512, [f"PA{b}"], ["RN"])
                            tt("dve", KT[:, sl], KT[:, sl], RN[:], ALU.mult, ["KT", "RN"], ["KT"])
                        for b in range(4):
                            act(SQ[:], QT[:, b * 512:(b + 1) * 512], AF.Square, ["QT"], ["SQ"])
                            for t4 in range(4):
                                t = b * 4 + t4
                                mm(PB[:, t:t + 1], SQ[:, t4 * 128:(t4 + 1) * 128], onesb[:, 0:1], True, True, ["SQ", "onesb"], ["PB0"])
                        rsqrt(RQ[:], PB[:, 0:NT], 1.0, 1e-6, NT, ["PB0"], ["RQ"])
                        ts("dve", SC2[:], RQ[:], 128 ** -0.5, None, ALU.mult, None, ["RQ"], ["SC2"])
                        tt("dve", SC1[:], SC2[:], EG[:, :, h], ALU.mult, ["SC2", "EG"], ["SC1"])
                        for t in range(NT):
                            tr(PAb[:, t * 128:(t + 1) * 128], KT[:, t * 128:(t + 1) * 128], identb[:], ["KT", "identb"], [f"PA{t // 8}"])
                        for t in range(NT):
                            tr(PAb[:, 2048 + t * 128:2048 + (t + 1) * 128], VT[:, t * 128:(t + 1) * 128], identb[:], ["VT", "identb"], [f"PA{2 + t // 8}"])
                        cp("dve", K_TM[:].rearrange("p t d -> p (t d)"), PAb[:, 0:2048], ["PA0", "PA1"], ["K_TM"])
                        cp("act", V_TM[:].rearrange("p t d -> p (t d)"), PAb[:, 2048:4096], ["PA2", "PA3"], ["V_TM"])
                        bc = lambda a: a[:, :, h:h + 1].to_broadcast([128, NT, 128])
                        tt("dve", KBG[:], K_TM[:], bc(BEG), ALU.mult, ["K_TM", "BEG"], ["KBG"])
                        tt("dve", K_TM[:], K_TM[:], bc(ETL), ALU.mult, ["K_TM", "ETL"], ["K_TM"])
                        tt("dve", V_TM[:], V_TM[:], bc(BETA), ALU.mult, ["V_TM", "BETA"], ["V_TM"])
                        ckpt(3)
                        S.op("dve", lambda e: e.memset(Sst[:], 0.0), writes=["Sst"])
                        S.op("dve", lambda e: e.memset(Sb[:], 0.0), writes=["Sb"])
                        fl = lambda a_: a_.rearrange("p t d -> p (t d)")

                        def prescan_chain(gi, h=h):
                            ub = gi % 2
                            for tl in range(4):
                                t = gi * 4 + tl
                                bi_ = tl % 2
                                tsl = slice(t * 128, (t + 1) * 128)
                                c = slice(tl * 128, (tl + 1) * 128)
                                c1 = slice(512 + tl * 128, 512 + (tl + 1) * 128)
                                c2 = slice(1024 + tl * 128, 1024 + (tl + 1) * 128)
                                c3 = slice(1536 + tl * 128, 1536 + (tl + 1) * 128)
                                ts("dve", NGL[:, bi_, 0, :], ltrib[:], GHL[:, 0, t, h:h + 1], None, ALU.mult, None, ["ltrib", "GHL0"], [f"NGL{bi_}"])
                                ts("dve", NGL[:, bi_, 1, :], ltrib[:], GHL[:, 1, t, h:h + 1], None, ALU.mult, None, ["ltrib", "GHL1"], [f"NGL{bi_}"])
                                mm(PA[:, c], KT[:, tsl], KT[:, tsl], True, True, ["KT"], ["PA0"])
                                mm(PA[:, c1], KT[:, tsl], QT[:, tsl], True, True, ["KT", "QT"], ["PA1"])
                                for cc, mk in ((c2, 0), (c3, 3)):
                                    bk = "PA2" if mk == 0 else "PA3"
                                    mm(PA[:, cc], onesb[:], NGL[:, bi_, 0, :], True, False, ["onesb", f"NGL{bi_}"], [bk])
                                    mm(PA[:, cc], onesb[:], NGL[:, bi_, 1, :], False, False, ["onesb", f"NGL{bi_}"], [bk])
                                    mm(PA[:, cc], identb[:], nmb[:, mk, :], False, True, ["identb", "nmb"], [bk])
                                act(ES[:, 0, c], PA[:, c2], AF.Exp, ["PA2", "GBt"], ["ES0"], bias=GBt[:, t, h:h + 1])
                                act(ES[:, 1, c], PA[:, c3], AF.Exp, ["PA3", "NGt"], ["ES1"], bias=NGt[:, t, h:h + 1], scale=-1.0)
                                yield
                            tt("dve", fl(CH[:, 0, 1]), PA[:, 0:512], ES[:, 0, :], ALU.mult, ["PA0", "ES0"], ["CH01a", "CH01b"])
                            tt("dve", fl(ATT[:, ub]), PA[:, 512:1024], ES[:, 1, :], ALU.mult, ["PA1", "ES1"], [f"ATT{ub}"])
                            for tl in range(4):
                                tr(PB[:, 512 + tl * 128:512 + (tl + 1) * 128], CH[:, 0, 1, tl, :], identf[:], ["CH01a", "CH01b", "c_identf"], ["PB1"])
                            yield
                            cp("act", fl(CH[:, 0, 0]), PB[:, 512:1024], ["PB1"], ["CH00a", "CH00b"])
                            tt("dve", CH[:, 0, 2], identf[:].unsqueeze(1).to_broadcast([128, 4, 128]), CH[:, 0, 0], ALU.subtract,
                               ["c_identf", "CH00a", "CH00b"], ["CH02a", "CH02b"])
                            yield
                            halves = (("a", 0, PB, "PB"), ("b", 2, PA, "PA"))
                            cur = 0
                            for lev in range(1, 7):
                                nxt = 1 - cur
                                last = lev == 6
                                for hn, t0, PP_, pn in halves:
                                    for tl2 in range(2):
                                        tl = t0 + tl2
                                        if not last:
                                            mm(PP_[:, 512 + tl2 * 128:512 + (tl2 + 1) * 128], CH[:, cur, 1, tl, :], CH[:, cur, 0, tl, :], True, True,
                                               [f"CH{cur}1{hn}", f"CH{cur}0{hn}"], [f"{pn}1"])
                                        mm(PP_[:, 1024 + tl2 * 128:1024 + (tl2 + 1) * 128], CH[:, cur, 0, tl, :], CH[:, cur, 1, tl, :], True, True,
                                           [f"CH{cur}1{hn}", f"CH{cur}0{hn}"], [f"{pn}2"])
                                yield
                                for hn, t0, PP_, pn in halves:
                                    cp("act", fl(CH[:, nxt, 1, t0:t0 + 2, :]), PP_[:, 1024:1280], [f"{pn}2"], [f"CH{nxt}1{hn}"])
                                    if not last:
                                        cp("act", fl(CH[:, nxt, 0, t0:t0 + 2, :]), PP_[:, 512:768], [f"{pn}1"], [f"CH{nxt}0{hn}"])
                                for hn, t0, PP_, pn in halves:
                                    for tl2 in range(2):
                                        tl = t0 + tl2
                                        mm(PP_[:, 1536 + tl2 * 128:1536 + (tl2 + 1) * 128], CH[:, nxt, 1, tl, :], CH[:, cur, 2, tl, :], True, True,
                                           [f"CH{nxt}1{hn}", f"CH{cur}2{hn}"], [f"{pn}3"])
                                yield
                                for hn, t0, PP_, pn in halves:
                                    dst_ = fl(TTb[:, t0:t0 + 2, :]) if last else fl(CH[:, nxt, 2, t0:t0 + 2, :])
                                    dk_ = f"TTb{hn}" if last else f"CH{nxt}2{hn}"
                                    tt("dve", dst_, PP_[:, 1536:1792], fl(CH[:, cur, 2, t0:t0 + 2, :]), ALU.add, [f"{pn}3", f"CH{cur}2{hn}"], [dk_])
                                cur = nxt
                            for tl in range(4):
                                t = gi * 4 + tl
                                hn = "a" if tl < 2 else "b"
                                mm(PA[:, tl * 128:(tl + 1) * 128], TTb[:, tl, :], V_TM[:, t, :], True, True, [f"TTb{hn}", "V_TM"], ["PA0"])
                                mm(PA[:, 512 + tl * 128:512 + (tl + 1) * 128], KBG[:, t, :], TTb[:, tl, :], True, True, [f"TTb{hn}", "KBG"], ["PA1"])
                            yield
                            cp("act", fl(U[:, ub]), PA[:, 0:512], ["PA0"], [f"U{ub}"])
                            cp("dve", fl(WT[:, ub]), PA[:, 512:1024], ["PA1"], [f"WT{ub}"])
                            yield

                        def scan(gi, h=h):
                            ub = gi % 2
                            for tl in range(4):
                                t = gi * 4 + tl
                                tsl = slice(t * 128, (t + 1) * 128)
                                mm(PB[:, 0:128], WT[:, ub, tl, :], Sb[:], True, True, [f"WT{ub}", "Sb"], ["PB0"])
                                tt("dve", VN[:], U[:, ub, tl, :], PB[:, 0:128], ALU.subtract, [f"U{ub}", "PB0"], ["VN"])
                                mm(PB[:, 128:256], QT[:, tsl], Sb[:], True, True, ["QT", "Sb"], ["PB0"])
                                yield
                                mm(PB[:, 256:384], ATT[:, ub, tl, :], VN[:], True, True, [f"ATT{ub}", "VN"], ["PB0"])
                                mm(PB[:, 384:512], K_TM[:, t, :], VN[:], True, True, ["K_TM", "VN"], ["PB0"])
                                stt(Sst[:], Sst[:], DL[:, t, h:h + 1], PB[:, 384:512], ALU.mult, ALU.add, ["Sst", "DL", "PB0"], ["Sst"])
                                cp("act", Sb[:], Sst[:], ["Sst"], ["Sb"])
                                yield
                                act(BS[:], PB[:, 256:384], AF.Copy, ["PB0", "SC2"], ["BS"], scale=SC2[:, t:t + 1])
                                stt(OT[:], PB[:, 128:256], SC1[:, t:t + 1], BS[:], ALU.mult, ALU.add, ["PB0", "SC1", "BS"], ["OT"])
                                S.op("dve", lambda e: e.scalar_tensor_tensor(OSQ[:], OT[:], 1.0, OT[:], ALU.mult, ALU.mult, accum_out=SS[:, 0:1]),
                                     reads=["OT"], writes=["OSQ", "SS"])
                                rsqrt(SS[:, 1:2], SS[:, 0:1], 1.0 / 128, 1e-6, 1, ["SS"], ["SS1"])
                                ts("dve", ONB[:, tl, :], OT[:], SS[:, 1:2], None, ALU.mult, None, ["OT", "SS1"], ["ONB"])
                                yield
                            for tl in range(4):
                                tr(PBb[:, tl * 128:(tl + 1) * 128], ONB[:, tl, :], identb[:], ["ONB", "identb"], ["PB0"])
                            ts("dve", OB[:, h, gi * 512:(gi + 1) * 512], PBb[:, 0:512], PS[:, 80:81], None, ALU.mult, None, ["PB0", "PSm"], [f"OB{h}"])
                            yield

                        for _ in prescan_chain(0):
                            pass
                        for gi in range(NT // 4):
                            gens = [scan(gi)]
                            if gi + 1 < NT // 4:
                                gens.append(prescan_chain(gi + 1))
                            while gens:
                                for g_ in list(gens):
                                    try:
                                        next(g_)
                                    except StopIteration:
                                        gens.remove(g_)
                        tt("dve", OB[:, h, :], OB[:, h, :], SZ, ALU.mult, [f"OB{h}", "XS"], [f"OB{h}"])
                    if l == 0:
                        dump("ODN", OB[:], [128, 4, SEQ], BF16, [f"OB{h}" for h in range(4)])
                S.barrier()
                ckpt(4)

                ORET = T(mx, "ORET", [128, 4, SEQ], BF16)
                with contextlib.ExitStack() as st:
                    rc = {k: T(st, "s" + k, CONST_SHAPES[k]) for k in RET_C}
                    for k in RET_C:
                        S.dma("sp", rc[k][:], cd[k], writes=[k])
                    NH = 4
                    WR = T(st, "WR", [128, 3, 8, 512], BF16)
                    QR = T(st, "QR", [128, 2, NH, 128], BF16)
                    KR = T(st, "KR", [128, 2, NH, 128], BF16)
                    KZ = T(st, "KZ", [128, 2, NH, 128], BF16)
                    VR = T(st, "VR", [128, 2, NH, 128], BF16)
                    T1 = T(st, "T1", [128, NH, 64])
                    T2 = T(st, "T2", [128, NH, 64])
                    T3 = T(st, "T3", [128, NH, 64])
                    T4 = T(st, "T4", [128, NH, 64])
                    RQT = T(st, "RQT", [128, NH, 128], BF16)
                    RKT = T(st, "RKT", [128, NH, 128], BF16)
                    QXI = T(st, "QXI", [128, NH, 128], BF16)
                    PP = T(st, "PP", [128, NH, 128], BF16)
                    OF = T(st, "OF", [128, NH, 128])
                    OQ = T(st, "OQ", [128, NH, 128])
                    ONr = T(st, "ONr", [128, NH, 128], BF16)
                    ST = T(st, "STr", [128, 6, NH])
                    SZb = T(st, "SZb", [128, 512], BF16)
                    f2 = lambda a_: a_.rearrange("p h d -> p (h d)")
                    W_ = NH * 128
                    for i3, c0 in enumerate((C_RQ, C_RK, C_RV)):
                        wload(WR[:, i3], win_cols(l, c0, 512), f"WR{i3}")
                    S.op("dve", lambda e: e.memset(Rst[:], 0.0), writes=["Rst"])
                    S.op("dve", lambda e: e.memset(Rb[:], 0.0), writes=["Rb"])
                    def A_pe(t):
                        tsl = slice(t * 128, (t + 1) * 128)
                        for i3 in range(3):
                            for kc in range(8):
                                mm(PA[:, i3 * 512:i3 * 512 + W_], HT[:, kc, tsl], WR[:, i3, kc, :], kc == 0, kc == 7,
                                   [f"WR{i3}", f"H{kc}"], [f"PA{i3}"])

                    def A_rot(t):
                        u = t % 2
                        cosb = rc["c_cos"][:, t, :].unsqueeze(1).to_broadcast([128, NH, 64])
                        sinb = rc["c_sin"][:, t, :].unsqueeze(1).to_broadcast([128, NH, 64])
                        for i3, dst, dk_ in ((0, QR, "QR"), (1, KR, "KR")):
                            src = PA[:, i3 * 512:i3 * 512 + W_].rearrange("p (h d) -> p h d", h=NH)
                            x1, x2 = src[:, :, 0:64], src[:, :, 64:128]
                            tt("dve", T1[:], x1, cosb, ALU.mult, [f"PA{i3}", "c_cos"], ["T1"])
                            tt("dve", T2[:], x2, sinb, ALU.mult, [f"PA{i3}", "c_sin"], ["T2"])
                            tt("dve", T3[:], x2, cosb, ALU.mult, [f"PA{i3}", "c_cos"], ["T3"])
                            tt("dve", T4[:], x1, sinb, ALU.mult, [f"PA{i3}", "c_sin"], ["T4"])
                            tt("pool", dst[:, u, :, 0:64], T1[:], T2[:], ALU.subtract, ["T1", "T2"], [f"{dk_}{u}"])
                            tt("pool", dst[:, u, :, 64:128], T3[:], T4[:], ALU.add, ["T3", "T4"], [f"{dk_}{u}"])
                        cp("act", f2(VR[:, u]), PA[:, 1024:1024 + W_], ["PA2"], [f"VR{u}"])
                        tt("dve", KZ[:, u], KR[:, u], rc["c_zeta"][:], ALU.mult, [f"KR{u}", "c_zeta"], [f"KZ{u}"])

                    def B1(t):
                        u = t % 2
                        for hl in range(NH):
                            tr(PBb[:, hl * 128:(hl + 1) * 128], QR[:, u, hl, :], identb[:], [f"QR{u}", "identb"], ["PB0"])
                            tr(PBb[:, W_ + hl * 128:W_ + (hl + 1) * 128], KR[:, u, hl, :], identb[:], [f"KR{u}", "identb"], ["PB0"])
                        cp("act", f2(RQT[:]), PBb[:, 0:W_], ["PB0"], ["RQT"])
                        cp("dve", f2(RKT[:]), PBb[:, W_:2 * W_], ["PB0"], ["RKT"])
                        tt("dve", QXI[:], RQT[:], rc["c_xi"][:], ALU.mult, ["RQT", "c_xi"], ["QXI"])
                        for hl in range(NH):
                            mm(PB[:, 512 + hl * 128:512 + (hl + 1) * 128], RKT[:, hl, :], RQT[:, hl, :], True, True, ["RKT", "RQT"], ["PB1"])
                        tt("dve", PP[:], PB[:, 512:512 + W_].rearrange("p (h d) -> p h d", h=NH), rc["c_rmask"][:], ALU.mult,
                           ["PB1", "c_rmask"], ["PP"])
                        for hl in range(NH):
                            mm(PB[:, 1024 + hl * 128:1024 + (hl + 1) * 128], PP[:, hl, :], VR[:, u, hl, :], True, False, ["PP", f"VR{u}"], ["PB2"])
                            mm(PB[:, 1024 + hl * 128:1024 + (hl + 1) * 128], QXI[:, hl, :], Rb[:, hl, :], False, True, ["QXI", "Rb"], ["PB2"])
                        for hl in range(NH):
                            mm(PB[:, 1536 + hl * 128:1536 + (hl + 1) * 128], KZ[:, u, hl, :], VR[:, u, hl, :], True, True, [f"KZ{u}", f"VR{u}"], ["PB3"])

                    def B2C1(t):
                        for hl in range(NH):
                            stt(Rst[:, hl, :], Rst[:, hl, :], rcd[hl], PB[:, 1536 + hl * 128:1536 + (hl + 1) * 128], ALU.mult, ALU.add,
                                ["Rst", "PB3"], ["Rst"])
                        cp("act", f2(Rb[:]), f2(Rst[:]), ["Rst"], ["Rb"])
                        cp("act", f2(OF[:]), PB[:, 1024:1024 + W_], ["PB2"], ["OF"])
                        act(OQ[:], OF[:], AF.Square, ["OF"], ["OQ"])
                        S.op("dve", lambda e: e.reduce_sum(ST[:, 0, :], OF[:], AX.X), reads=["OF"], writes=["ST0"])
                        S.op("dve", lambda e: e.reduce_sum(ST[:, 1, :], OQ[:], AX.X), reads=["OQ"], writes=["ST1"])
                        ts("dve", ST[:, 2, :], ST[:, 0, :], 1.0 / 128, None, ALU.mult, None, ["ST0"], ["ST2"])
                        tt("dve", ST[:, 3, :], ST[:, 2, :], ST[:, 2, :], ALU.mult, ["ST2"], ["ST3"])
                        stt(ST[:, 4, :], ST[:, 1, :], 1.0 / 128, ST[:, 3, :], ALU.mult, ALU.subtract, ["ST1", "ST3"], ["ST4"])
                        ts("dve", ST[:, 4, :], ST[:, 4, :], 0.0, None, ALU.max, None, ["ST4"], ["ST4"])
                        rsqrt(ST[:, 5, :], ST[:, 4, :], 1.0, 1e-5, NH, ["ST4"], ["ST5"])
                        for hl in range(NH):
                            ts("dve", ONr[:, hl, :], OF[:, hl, :], ST[:, 2, hl:hl + 1], ST[:, 5, hl:hl + 1], ALU.subtract, ALU.mult,
                               ["OF", "ST2", "ST5"], ["ONr"])

                    def C2(t):
                        tsl = slice(t * 128, (t + 1) * 128)
                        for hl in range(NH):
                            tr(PBb[:, 1024 + hl * 128:1024 + (hl + 1) * 128], ONr[:, hl, :], identb[:], ["ONr", "identb"], ["PB1"])
                        for hl in range(NH):
                            ts("dve", ORET[:, hl, tsl], PBb[:, 1024 + hl * 128:1024 + (hl + 1) * 128], PS[:, 84 + hl:85 + hl], None, ALU.mult, None,
                               ["PB1", "PSm"], [f"OR{hl}"])

                    A_pe(0)
                    A_rot(0)
                    for t in range(NT):
                        B1(t)
                        if t + 1 < NT:
                            A_pe(t + 1)
                        B2C1(t)
                        if t + 1 < NT:
                            A_rot(t + 1)
                        C2(t)
                    for h in range(4):
                        wi = h % 2
                        wload(WR[:, wi, :, 0:128], win_cols(l, C_RG + h * 128, 128), f"WR{wi}")
                        proj_fm(WR[:, wi, :, 0:128], f"WR{wi}", PA, "PA")
                        for b in range(4):
                            bsl = slice(b * 512, (b + 1) * 512)
                            act(SZb[:], PA[:, bsl], AF.Silu, [f"PA{b}"], ["SZb"])
                            tt("dve", ORET[:, h, bsl], ORET[:, h, bsl], SZb[:], ALU.mult, [f"OR{h}", "SZb"], [f"OR{h}"])
                    if l == 0:
                        dump("ORET", ORET[:], [128, 4, SEQ], BF16, [f"OR{h}" for h in range(4)])
                S.barrier()

                Y = T(mx, "Y", [128, 8, SEQ], BF16)

                def merge(br, wname, SRC, skey, first=False):
                    with contextlib.ExitStack() as st:
                        WG = T(st, "WG", [128, 2, 8, 128], BF16)
                        WB = T(st, "WB", [128, 2, 4, 128], BF16)
                        SG = T(st, "SG", [128, SEQ], BF16)
                        TM = T(st, "TMg", [128, SEQ], BF16)
                        for f in range(8):
                            wi = f % 2
                            wload(WG[:, wi], win_cols(l, C_GT + br * 1024 + f * 128, 128), f"WG{wi}")
                            wload(WB[:, wi], wd[wname][l].rearrange("(k p) n -> p k n", p=128)[:, :, f * 128:(f + 1) * 128], f"WB{wi}")
                            proj_fm(WG[:, wi], f"WG{wi}", PA, "PA")
                            for kc in range(4):
                                for b in range(4):
                                    mm(PB[:, b * 512:(b + 1) * 512], WB[:, wi, kc, :], SRC[:, kc, b * 512:(b + 1) * 512], kc == 0, kc == 3,
                                       [f"WB{wi}", f"{skey}{kc}"], [f"PB{b}"])
                            act(SG[:], PA[:], AF.Sigmoid, bks("PA", 0, 2048), ["SG"])
                            if first:
                                tt("dve", Y[:, f, :], PB[:], SG[:], ALU.mult, bks("PB", 0, 2048) + ["SG"], [f"Y{f}"])
                            else:
                                tt("dve", TM[:], PB[:], SG[:], ALU.mult, bks("PB", 0, 2048) + ["SG"], ["TMg"])
                                tt("dve", Y[:, f, :], Y[:, f, :], TM[:], ALU.add, [f"Y{f}", "TMg"], [f"Y{f}"])
                    S.barrier()

                merge(0, "w_branch_dn", OB, "OB", first=True)
                ckpt(5)
                merge(2, "w_branch_ret", ORET, "OR")

                with contextlib.ExitStack() as st:
                    WD = T(st, "WDp", [128, 2, 8, 128], BF16)
                    PW = T(st, "PW", [128, 4, 128], BF16)
                    UU = T(st, "UU", [128, 16 + SEQ])
                    SA = T(st, "SA", [128, 16 + SEQ])
                    SBb = T(st, "SBb", [128, 16 + SEQ])
                    MX = T(st, "MX", [128, SEQ], BF16)
                    wload(PW[:], wd["pool_w"][l].rearrange("g c d -> c g d"), "PW")
                    for a_, k_ in ((UU, "UU"), (SA, "SA"), (SBb, "SBb")):
                        S.op("dve", lambda e: e.memset(a_[:, 0:16], 0.0), writes=[k_])
                    for gq, win in enumerate((2, 4, 8, 16)):
                        wi = gq % 2
                        wload(WD[:, wi], win_cols(l, C_PU + gq * 128, 128), f"WDp{wi}")
                        proj_fm(WD[:, wi], f"WDp{wi}", PA, "PA")
                        cp("act", UU[:, 16:], PA[:], bks("PA", 0, 2048), ["UU"])
                        src, sk = UU, "UU"
                        bufs = [(SA, "SA"), (SBb, "SBb")]
                        sh, bi = 1, 0
                        while sh < win:
                            dstb, dkk = bufs[bi]
                            tt("dve", dstb[:, 16:], src[:, 16:], src[:, 16 - sh:16 + SEQ - sh], ALU.add, [sk], [dkk])
                            src, sk = dstb, dkk
                            bi = 1 - bi
                            sh *= 2
                        stt(MX[:], src[:, 16:], 1.0 / win, UU[:, 16:], ALU.mult, ALU.subtract, [sk, "UU"], ["MX"])
                        dstb, dkk = bufs[bi]
                        tt("dve", dstb[:, 16:32], src[:, 16:32], cs["c_pinv"][:, gq, :], ALU.mult, [sk, "c_pinv"], [dkk])
                        tt("dve", MX[:, 0:16], dstb[:, 16:32], UU[:, 16:32], ALU.subtract, [dkk, "UU"], ["MX"])
                        for b in range(4):
                            mm(PB[:, b * 512:(b + 1) * 512], PW[:, gq, :], MX[:, b * 512:(b + 1) * 512], True, True, ["PW", "MX"], [f"PB{b}"])
                        ts("dve", OB[:, gq, :], PB[:], PS[:, 96 + gq:97 + gq], None, ALU.mult, None, bks("PB", 0, 2048) + ["PSm"], [f"OB{gq}"])
                    if l == 0:
                        dump("OPOOL", OB[:], [128, 4, SEQ], BF16, [f"OB{h}" for h in range(4)])
                S.barrier()
                merge(1, "w_branch_pool", OB, "OB")
                ckpt(6)

                ckpt(7)
                if l == 0:
                    dump("Y", Y[:], [128, 8, SEQ], BF16, [f"Y{f}" for f in range(8)])

                with contextlib.ExitStack() as st:
                    BW = 512
                    WO = OB[:].rearrange("p h s -> p (h s)").rearrange("p (k n) -> p k n", k=8)
                    M2 = HT[:].rearrange("p f s -> p (f s)").bitcast(F32).rearrange("p (u f s) -> p u f s", u=2, f=8)
                    SQ = T(st, "SQo", [128, 2, BW], BF16)
                    RS = T(st, "RSo", [128, BW])
                    TMo = T(st, "TMo", [128, BW])
                    wload(WO[:, 0:4, :], wd["w_out"][l].rearrange("(k p) n -> p k n", p=128)[:, 0:4, :], "WOa")
                    wload(WO[:, 4:8, :], wd["w_out"][l].rearrange("(k p) n -> p k n", p=128)[:, 4:8, :], "WOb")
                    for b in range(SEQ // BW):
                        bsl = slice(b * BW, (b + 1) * BW)
                        M = M2[:, b % 2]
                        mk_ = f"M{b % 2}_"
                        for f in range(8):
                            pr = PA[:, (f % 4) * 512:(f % 4) * 512 + BW]
                            for kc in range(8):
                                mm(pr, WO[:, kc, f * 128:(f + 1) * 128], Y[:, kc, bsl], kc == 0, kc == 7,
                                   ["WOa" if kc < 4 else "WOb", f"Y{kc}"], [f"PA{f % 4}"])
                            cp("act", M[:, f, :], pr, [f"PA{f % 4}"], [mk_ + str(f)])
                            act(SQ[:, f % 2, :], M[:, f, :], AF.Square, [mk_ + str(f)], [f"SQo{f % 2}"])
                            mm(PB[:, 0:BW], onesb[:], SQ[:, f % 2, :], f == 0, f == 7, ["onesb", f"SQo{f % 2}"], ["PB0"])
                        rsqrt(RS[:], PB[:, 0:BW], 1.0 / D, 1e-6, BW, ["PB0"], ["RSo"])
                        for f in range(8):
                            stt(TMo[:], M[:, f, :], PS[:, 8 + f:9 + f], RS[:], ALU.mult, ALU.mult, [mk_ + str(f), "RSo", "PSm"], ["TMo"])
                            tt("dve", X[:, f, bsl], X[:, f, bsl], TMo[:], ALU.add, [f"X{f}", "TMo"], [f"X{f}"])
                S.barrier()
            S.barrier()
            if l == 0:
                dump("XMIX", X[:], [128, 8, SEQ], F32, [f"X{f}" for f in range(8)])
            ckpt(8)

            with contextlib.ExitStack() as st:
                HTh = T(st, "HTh", [128, 8, 1024], BF16)
                A = T(st, "A", [128, NJ, 1024], BF16)
                M = T(st, "Mf", [128, 8, 1024])
                WG = T(st, "WGf", [128, 2, 8, 256], BF16)
                WU = T(st, "WUf", [128, 2, 8, 256], BF16)
                WDn = T(st, "WDn", [128, 2, NJ, 128], BF16)
                SGt = T(st, "SGt", [128, 1024], BF16)
                SQ = T(st, "SQf", [128, 2, 512], BF16)
                RS = T(st, "RSf", [128, 2, 512])
                TMf = T(st, "TMf", [128, 512])

                def gen_pre(hf):
                    for bi, blk in enumerate((2 * hf, 2 * hf + 1)):
                        bsl = slice(blk * 512, (blk + 1) * 512)
                        for f in range(8):
                            act(SQ[:, f % 2, :], X[:, f, bsl], AF.Square, [f"X{f}"], [f"SQf{f % 2}"])
                            mm(PB[:, 1536:2048], onesb[:], SQ[:, f % 2, :], f == 0, f == 7, ["onesb", f"SQf{f % 2}"], ["PB3"])
                            if f % 2 == 1:
                                yield
                        rsqrt(RS[:, 0, :], PB[:, 1536:2048], 1.0 / D, 1e-6, 512, ["PB3"], ["RSf0"])
                        yield
                        for f in range(8):
                            stt(HTh[:, f, bi * 512:(bi + 1) * 512], X[:, f, bsl], PS[:, 16 + f:17 + f], RS[:, 0, :], ALU.mult, ALU.mult,
                                [f"X{f}", "RSf0", "PSm"], [f"Hh{f}"])
                            if f % 2 == 1:
                                yield

                def gate_up(hf, side=None):
                    for jc in range(11):
                        wi = jc % 2
                        wload(WG[:, wi], wd["ffn_gate"][l].rearrange("(k p) n -> p k n", p=128)[:, :, jc * 256:jc * 256 + 256], f"WGf{wi}")
                        wload(WU[:, wi], wd["ffn_up"][l].rearrange("(k p) n -> p k n", p=128)[:, :, jc * 256:jc * 256 + 256], f"WUf{wi}")
                        for jj in range(2):
                            j = jc * 2 + jj
                            po = (j % 2) * 1024
                            for kc in range(8):
                                for b2 in range(2):
                                    mm(PA[:, po + b2 * 512:po + (b2 + 1) * 512], WG[:, wi, kc, jj * 128:(jj + 1) * 128], HTh[:, kc, b2 * 512:(b2 + 1) * 512],
                                       kc == 0, kc == 7, [f"WGf{wi}", f"Hh{kc}"], [f"PA{(po // 512) + b2}"])
                            for kc in range(8):
                                for b2 in range(2):
                                    mm(PB[:, po + b2 * 512:po + (b2 + 1) * 512], WU[:, wi, kc, jj * 128:(jj + 1) * 128], HTh[:, kc, b2 * 512:(b2 + 1) * 512],
                                       kc == 0, kc == 7, [f"WUf{wi}", f"Hh{kc}"], [f"PB{(po // 512) + b2}"])
                            act(SGt[:], PA[:, po:po + 1024], AF.Silu, bks("PA", po, po + 1024), ["SGt"])
                            tt("dve", A[:, j, :], PB[:, po:po + 1024], SGt[:], ALU.mult, bks("PB", po, po + 1024) + ["SGt"], [f"A{j}"])
                            if side is not None:
                                for _ in range(2):
                                    next(side, None)

                def down(hf, side=None):
                    for f in range(8):
                        wi = f % 2
                        wload(WDn[:, wi], wd["ffn_down"][l].rearrange("(j p) n -> p j n", p=128)[:, :, f * 128:(f + 1) * 128], f"WDn{wi}")
                        po = (f % 2) * 1024
                        for j in range(NJ):
                            for b2 in range(2):
                                mm(PA[:, po + b2 * 512:po + (b2 + 1) * 512], WDn[:, wi, j, :], A[:, j, b2 * 512:(b2 + 1) * 512], j == 0, j == NJ - 1,
                                   [f"WDn{wi}", f"A{j}"], [f"PA{(po // 512) + b2}"])
                        if side is not None:
                            for _ in range(3):
                                next(side, None)
                        cp("act", M[:, f, :], PA[:, po:po + 1024], bks("PA", po, po + 1024), [f"Mf{f}"])
                        for b2 in range(2):
                            act(SQ[:, b2, :], M[:, f, b2 * 512:(b2 + 1) * 512], AF.Square, [f"Mf{f}"], [f"SQf{b2}"])
                            mm(PB[:, b2 * 512:(b2 + 1) * 512], onesb[:], SQ[:, b2, :], f == 0, f == 7, ["onesb", f"SQf{b2}"], [f"PB{b2}"])
                    if side is not None:
                        for _ in side:
                            pass
                    for b2 in range(2):
                        rsqrt(RS[:, b2, :], PB[:, b2 * 512:(b2 + 1) * 512], 1.0 / D, 1e-6, 512, [f"PB{b2}"], [f"RSf{b2}"])

                def gen_post(hf):
                    for b2 in range(2):
                        hsl = slice(hf * 1024 + b2 * 512, hf * 1024 + (b2 + 1) * 512)
                        for f in range(8):
                            stt(TMf[:], M[:, f, b2 * 512:(b2 + 1) * 512], PS[:, 24 + f:25 + f], RS[:, b2, :], ALU.mult, ALU.mult,
                                [f"Mf{f}", f"RSf{b2}", "PSm"], ["TMf"])
                            tt("dve", X[:, f, hsl], X[:, f, hsl], TMf[:], ALU.add, [f"X{f}", "TMf"], [f"X{f}"])
                            yield

                for _ in gen_pre(0):
                    pass
                gate_up(0)
                down(0, side=gen_pre(1))
                gp = gen_post(0)
                gate_up(1, side=gp)
                for _ in gp:
                    pass
                down(1)
                for _ in gen_post(1):
                    pass
            S.barrier()

        except _Stop:
            pass
        S.off = False
        for f in range(8):
            S.dma("sp", yT[f * 128:(f + 1) * 128, :], X[:, f, :], reads=[f"X{f}"])
        S.barrier()
    return nc, dbg_outs


def pack_small(inp, l):
    ps = np.zeros((128, NSM), np.float32)
    fm = lambda v: np.asarray(v, np.float32).reshape(-1, 128).T
    ps[:, 0:8] = fm(inp["mix_pre_norm"][l])
    ps[:, 8:16] = fm(inp["mix_post_norm"][l])
    ps[:, 16:24] = fm(inp["ffn_pre_norm"][l])
    ps[:, 24:32] = fm(inp["ffn_post_norm"][l])
    cv = np.asarray(inp["dn_conv"][l], np.float32)
    ps[:, 32:80] = cv.reshape(4, 12, 128).transpose(2, 1, 0).reshape(128, 48)
    ps[:, 80] = np.asarray(inp["dn_out_norm"][l], np.float32)
    ps[:, 84:88] = fm(inp["ret_out_norm"][l])
    ps[:, 96:100] = fm(inp["pool_scale"][l])
    ps[:, 100:164] = np.tile(np.asarray(inp["dn_A_log"][l], np.float32)[None, :], (128, NT))
    ps[:, 164:228] = np.tile(np.asarray(inp["dn_dt_bias"][l], np.float32)[None, :], (128, NT))
    return ps


_CACHE = {}


def run_layers(inp, x_fm_list, layers, dbg=False):
    consts = host_consts()
    rcd = consts.pop("_rcd")
    NL = len(layers)
    key = (NL, dbg)
    if key not in _CACHE:
        _CACHE[key] = build_program(NL, rcd, dbg)
    nc, dbg_outs = _CACHE[key]
    shared = {k: v for k, v in consts.items()}
    shared["p_small"] = np.stack([pack_small(inp, l) for l in layers])
    for k in W_SHAPES:
        shared[k] = np.ascontiguousarray(np.asarray(inp[k], np.float32)[layers[0]:layers[-1] + 1])
    in_maps = [dict(shared, xT=np.ascontiguousarray(x)) for x in x_fm_list]
    res = run_bass_kernel_spmd(nc, in_maps, core_ids=list(range(len(x_fm_list))))
    return res


FUSED = True


def kernel(**inputs):
    x = np.asarray(inputs["x"], np.float32)
    xs = [np.ascontiguousarray(x[b].T) for b in range(x.shape[0])]
    if FUSED:
        res = run_layers(inputs, xs, [0, 1, 2, 3])
        xs = [r["yT"] for r in res.results]
    else:
        for l in range(4):
            res = run_layers(inputs, xs, [l])
            xs = [np.asarray(r["yT"], np.float32) for r in res.results]
    return np.stack([np.asarray(y, np.float32).T for y in xs]).astype(np.float32)
```

```python
import contextlib
import numpy as np
import ml_dtypes
import concourse.bass as bass
import concourse.mybir as mybir
from concourse.bass_utils import run_bass_kernel_spmd

F32 = mybir.dt.float32
BF16 = mybir.dt.bfloat16
AF = mybir.ActivationFunctionType
ALU = mybir.AluOpType
AX = mybir.AxisListType

SEQ = 2048
D = 1024
NT = 16
DFF = 2816
NJ = 22
D_IN = 7688
C_Q, C_K, C_V, C_Z, C_BA = 0, 512, 1024, 1536, 2048
C_RQ, C_RK, C_RV, C_RG, C_PU, C_GT = 2056, 2568, 3080, 3592, 4104, 4616
NSM = 228
NEG = -30000.0


class Sched:
    def __init__(self, nc, n_dma_slots=6):
        self.nc = nc
        self.engs = {"pe": nc.tensor, "dve": nc.vector, "act": nc.scalar, "pool": nc.gpsimd, "sp": nc.sync}
        self.sem = {k: nc.alloc_semaphore(name="c_" + k) for k in self.engs}
        self.cnt = {k: 0 for k in self.engs}
        self.seen = {k: {} for k in self.engs}
        self.dma_sems, self.dma_cnt, self.dma_pos = {}, {}, {}
        self.n_dma_slots = n_dma_slots
        self.lastw, self.readers = {}, {}
        self.off = False

    def _wait(self, e, tok):
        sem, val, src = tok
        if src == e and e == "pe":
            return
        if src == e and self.cnt[e] - val >= 3:
            return
        key = id(sem)
        if self.seen[e].get(key, 0) >= val:
            return
        self.engs[e].wait_ge(sem, val)
        self.seen[e][key] = val

    def _deps(self, e, reads, writes):
        for k in reads:
            t = self.lastw.get(k)
            if t is not None:
                self._wait(e, t)
        for k in writes:
            t = self.lastw.get(k)
            if t is not None:
                self._wait(e, t)
            for t in self.readers.get(k, ()):
                self._wait(e, t)

    def _record(self, tok, reads, writes):
        for k in writes:
            self.lastw[k] = tok
            self.readers[k] = []
        for k in reads:
            self.readers.setdefault(k, []).append(tok)

    @staticmethod
    def _norm(reads, writes):
        r2, w2 = [], []
        for k in reads:
            if k[:2] in ("PA", "PB") and len(k) >= 3 and k[2].isdigit():
                w2.append(k[:3])
            else:
                r2.append(k)
        for k in writes:
            if k[:2] in ("PA", "PB") and len(k) >= 3 and k[2].isdigit():
                w2.append(k[:3])
            else:
                w2.append(k)
        return r2, w2

    def op(self, e, fn, reads=(), writes=()):
        if self.off:
            return
        reads, writes = self._norm(reads, writes)
        self._deps(e, reads, writes)
        ins = fn(self.engs[e])
        self.cnt[e] += 1
        ins.then_inc(self.sem[e], 1)
        self._record((self.sem[e], self.cnt[e], e), reads, writes)

    def dma(self, q, out, in_, reads=(), writes=()):
        if self.off:
            return
        if q not in self.dma_sems:
            self.dma_sems[q] = [self.nc.alloc_semaphore(name=f"d_{q}_{i}") for i in range(self.n_dma_slots)]
            self.dma_cnt[q] = [0] * self.n_dma_slots
            self.dma_pos[q] = 0
        s = self.dma_pos[q]
        self.dma_pos[q] = (s + 1) % self.n_dma_slots
        sem = self.dma_sems[q][s]
        if self.dma_cnt[q][s] > 0:
            self._wait(q, (sem, 16 * self.dma_cnt[q][s], "dma"))
        self._deps(q, reads, writes)
        ins = self.engs[q].dma_start(out=out, in_=in_)
        self.dma_cnt[q][s] += 1
        ins.then_inc(sem, 16)
        self._record((sem, 16 * self.dma_cnt[q][s], "dma"), reads, writes)

    def barrier(self):
        if self.off:
            return
        toks = [(self.sem[e], self.cnt[e], e) for e in self.engs if self.cnt[e] > 0]
        for q in self.dma_sems:
            for s, sem in enumerate(self.dma_sems[q]):
                if self.dma_cnt[q][s] > 0:
                    toks.append((sem, 16 * self.dma_cnt[q][s], "dma"))
        for e in self.engs:
            for t in toks:
                if t[2] != e:
                    self._wait(e, t)
        self.lastw, self.readers = {}, {}


def host_consts():
    c = {}
    i = np.arange(128)
    c["c_identf"] = np.eye(128, dtype=np.float32)
    c["c_onesf"] = np.ones((128, 128), np.float32)
    c["c_ltri"] = (i[:, None] <= i[None, :]).astype(np.float32)
    nm = np.zeros((128, 4, 128), np.float32)
    nm[:, 0, :] = np.where(i[:, None] > i[None, :], 0.0, NEG)
    nm[:, 1, :] = np.where(i[None, :] > i[:, None], 0.0, NEG)
    nm[:, 2, :] = np.where(i[None, :] >= i[:, None], 0.0, NEG)
    nm[:, 3, :] = np.where(i[None, :] >= i[:, None], 0.0, -NEG)
    c["c_nm"] = nm
    lg = np.log1p(-np.exp2(-5.0 - np.arange(4))).astype(np.float64)
    same = (i[:, None] // 64) == (i[None, :] // 64)
    lower = (i[:, None] // 64) > (i[None, :] // 64)
    rmask = np.zeros((128, 4, 128), np.float64)
    xi = np.zeros((128, 4, 128), np.float64)
    zeta = np.zeros((128, 4, 128), np.float64)
    for h in range(4):
        gam = np.exp(lg[h])
        M = np.where(same, gam ** np.abs(i[:, None] - i[None, :]),
                     np.where(lower, gam ** np.clip(i[:, None] - i[None, :], 0, None), 0.0)) * 128 ** -0.5
        rmask[:, h, :] = M.T
        xi[:, h, :] = (gam ** (i + 1.0))[None, :]
        zeta[:, h, :] = (gam ** (127.0 - i) * 128 ** -0.5)[:, None]
    c["c_rmask"] = rmask.astype(np.float32)
    c["c_xi"] = xi.astype(np.float32)
    c["c_zeta"] = zeta.astype(np.float32)
    c["_rcd"] = [float(np.exp(lg[h]) ** 128) for h in range(4)]
    half = 64
    inv = 10000.0 ** (-np.arange(half, dtype=np.float32) / half)
    ang = np.arange(SEQ, dtype=np.float32)[:, None] * inv[None, :]
    cos = np.cos(ang).astype(np.float32).reshape(NT, 128, 64).transpose(1, 0, 2)
    sin = np.sin(ang).astype(np.float32).reshape(NT, 128, 64).transpose(1, 0, 2)
    c["c_cos"] = np.ascontiguousarray(cos)
    c["c_sin"] = np.ascontiguousarray(sin)
    l0 = np.zeros((128, 3, 4, NT, 2), np.float32)
    l0[0, 0, :, :, 1] = 1.0
    l0[0, 2, :, :, 0] = 1.0
    c["c_l0"] = l0
    pinv = np.zeros((128, 4, 16), np.float32)
    for g, w in enumerate((2, 4, 8, 16)):
        pinv[:, g, :] = 1.0 / np.minimum(np.arange(1, 17), w)
    c["c_pinv"] = pinv
    return c


CONST_SHAPES = {
    "c_identf": [128, 128], "c_onesf": [128, 128], "c_ltri": [128, 128], "c_nm": [128, 4, 128],
    "c_rmask": [128, 4, 128], "c_xi": [128, 4, 128], "c_zeta": [128, 4, 128],
    "c_cos": [128, NT, 64], "c_sin": [128, NT, 64], "c_l0": [128, 3, 4, NT, 2], "c_pinv": [128, 4, 16],
}
W_SHAPES = {
    "w_in": [D, D_IN], "pool_w": [4, 128, 128], "w_branch_dn": [512, D], "w_branch_ret": [512, D],
    "w_branch_pool": [512, D], "w_out": [D, D], "ffn_gate": [D, DFF], "ffn_up": [D, DFF], "ffn_down": [DFF, D],
}


class _Stop(Exception):
    pass


STOP = [99]


def build_program(NL, rcd, dbg=False):
    nc = bass.Bass("TRN2", target_bir_lowering=False)
    S = Sched(nc)
    xT = nc.dram_tensor("xT", [D, SEQ], F32, kind="ExternalInput").ap()
    yT = nc.dram_tensor("yT", [D, SEQ], F32, kind="ExternalOutput").ap()
    psmall = nc.dram_tensor("p_small", [NL, 128, NSM], F32, kind="ExternalInput").ap()
    cd = {k: nc.dram_tensor(k, shp, F32, kind="ExternalInput").ap() for k, shp in CONST_SHAPES.items()}
    wd = {k: nc.dram_tensor(k, [NL] + shp, F32, kind="ExternalInput").ap() for k, shp in W_SHAPES.items()}
    dbg_outs = {}

    def mm(out, lhsT, rhs, start, stop, r, w):
        S.op("pe", lambda e: e.matmul(out, lhsT, rhs, start=start, stop=stop), reads=r, writes=w)

    def tr(out, in_, ident, r, w):
        S.op("pe", lambda e: e.transpose(out, in_, ident), reads=r, writes=w)

    def act(out, in_, func, r, w, bias=None, scale=None):
        kw = {}
        if bias is not None:
            kw["bias"] = bias
        if scale is not None:
            kw["scale"] = scale
        S.op("act", lambda e: e.activation(out, in_, func, **kw), reads=r, writes=w)

    def tt(eng, out, in0, in1, op, r, w):
        S.op(eng, lambda e: e.tensor_tensor(out, in0, in1, op), reads=r, writes=w)

    def ts(eng, out, in0, s1, s2, op0, op1, r, w):
        if s2 is None:
            S.op(eng, lambda e: e.tensor_scalar(out, in0, s1, None, op0), reads=r, writes=w)
        else:
            S.op(eng, lambda e: e.tensor_scalar(out, in0, s1, s2, op0, op1), reads=r, writes=w)

    def stt(out, in0, sc, in1, op0, op1, r, w):
        S.op("dve", lambda e: e.scalar_tensor_tensor(out, in0, sc, in1, op0, op1), reads=r, writes=w)

    def cp(eng, out, in_, r, w):
        if eng == "act":
            S.op("act", lambda e: e.copy(out, in_), reads=r, writes=w)
        else:
            S.op(eng, lambda e: e.tensor_copy(out, in_), reads=r, writes=w)

    def bks(name, c0, c1):
        return [f"{name}{b}" for b in range(c0 // 512, (c1 - 1) // 512 + 1)]

    with contextlib.ExitStack() as g:
        tcount = [0]

        def T(st, name, shape, dt=F32):
            tcount[0] += 1
            return st.enter_context(nc.sbuf_tensor(f"{name}_{tcount[0]}", shape, dt))

        PA = g.enter_context(nc.psum_tensor("PA", [128, 2048], F32))
        PB = g.enter_context(nc.psum_tensor("PB", [128, 2048], F32))
        PAb = PA[:].bitcast(BF16)
        PBb = PB[:].bitcast(BF16)
        X = T(g, "X", [128, 8, SEQ])
        RET_C = ("c_rmask", "c_xi", "c_zeta", "c_cos", "c_sin")
        cs = {k: T(g, "s" + k, shp) for k, shp in CONST_SHAPES.items() if k != "c_l0" and k not in RET_C}
        identb = T(g, "identb", [128, 128], BF16)
        onesb = T(g, "onesb", [128, 128], BF16)
        nmb = T(g, "nmb", [128, 4, 128], BF16)
        ltrib = T(g, "ltrib", [128, 128], BF16)
        LABC = T(g, "LABC", [128, 3, 4, NT, 2])
        PS = T(g, "PSm", [128, NSM])
        Rst = T(g, "Rst", [128, 4, 128])
        Rb = T(g, "Rb", [128, 4, 128], BF16)
        Sst = T(g, "Sst", [128, 128])
        Sb = T(g, "Sb", [128, 128], BF16)

        for k in cs:
            S.dma("sp", cs[k][:], cd[k], writes=[k])
        S.dma("sp", LABC[:], cd["c_l0"], writes=["LABC"])
        for f in range(8):
            S.dma("sp", X[:, f, :], xT[f * 128:(f + 1) * 128, :], writes=[f"X{f}"])
        cp("dve", identb[:], cs["c_identf"][:], ["c_identf"], ["identb"])
        cp("dve", onesb[:], cs["c_onesf"][:], ["c_onesf"], ["onesb"])
        cp("dve", nmb[:], cs["c_nm"][:], ["c_nm"], ["nmb"])
        cp("dve", ltrib[:], cs["c_ltri"][:], ["c_ltri"], ["ltrib"])
        identf, onesf, ltri = cs["c_identf"], cs["c_onesf"], cs["c_ltri"]

        def rsqrt(out, in_, scale, eps, n, r, w, shape3=None):
            act(out, in_, AF.Ln, r, w, bias=eps, scale=scale)
            act(out, out, AF.Exp, list(w), w, scale=-0.5)

        def dump(name, ap, shape, dt, keys):
            if not dbg:
                return
            o = nc.dram_tensor("dbg_" + name, shape, dt, kind="ExternalOutput").ap()
            S.dma("sp", o, ap, reads=keys)
            dbg_outs[name] = (shape, dt)

        def ckpt(stage):
            if STOP[0] == stage and not S.off:
                S.barrier()
                S.off = True

        wcount = [0]

        import os
        skipw = os.environ.get("SKIPW", "")

        def wload(dst, src, key):
            if skipw and key[:3] in skipw.split(","):
                return
            S.dma("pool", dst, src, writes=[key])

        def win_cols(l, c0, w):
            return wd["w_in"][l].rearrange("(k p) n -> p k n", p=128)[:, :, c0:c0 + w]

        def rmsnorm(src_fn, src_keys, wcol0, dst_fn, dst_keys, blks, st):
            SQ = T(st, "n_SQ", [128, 2, 512], BF16)
            RS = T(st, "n_RS", [128, 512])
            for bi, blk in enumerate(blks):
                for f in range(8):
                    act(SQ[:, f % 2, :], src_fn(f, blk), AF.Square, [src_keys(f)], [f"nSQ{f % 2}"])
                    mm(PB[:, 1536:2048], onesb[:], SQ[:, f % 2, :], f == 0, f == 7, ["onesb", f"nSQ{f % 2}"], ["PB3"])
                rsqrt(RS[:], PB[:, 1536:2048], 1.0 / D, 1e-6, 512, ["PB3"], ["nRS"])
                for f in range(8):
                    stt(dst_fn(f, bi), src_fn(f, blk), PS[:, wcol0 + f:wcol0 + f + 1], RS[:], ALU.mult, ALU.mult,
                        [src_keys(f), "nRS", "PSm"], [dst_keys(f)])

        def postnorm_residual(M, blk, wcol0, st_keys):
            pass

        try:
          for l in range(NL):
            S.dma("sp", PS[:], psmall[l], writes=["PSm"])
            with contextlib.ExitStack() as mx:
                HT = T(mx, "HT", [128, 8, SEQ], BF16)
                OB = T(mx, "OB", [128, 4, SEQ], BF16)
                with contextlib.ExitStack() as st:
                    rmsnorm(lambda f, b: X[:, f, b * 512:(b + 1) * 512], lambda f: f"X{f}", 0,
                            lambda f, bi: HT[:, f, bi * 512:(bi + 1) * 512], lambda f: f"H{f}", range(4), st)
                S.barrier()
                if l == 0:
                    dump("HT", HT[:], [128, 8, SEQ], BF16, [f"H{f}" for f in range(8)])
                ckpt(0)

                def proj_fm(Wt, wkey, dst, dkey):
                    for kc in range(8):
                        for b in range(4):
                            mm(dst[:, b * 512:(b + 1) * 512], Wt[:, kc, :], HT[:, kc, b * 512:(b + 1) * 512],
                               kc == 0, kc == 7, [wkey, f"H{kc}"], [f"{dkey}{b}"])

                with contextlib.ExitStack() as st:
                    WD = T(st, "WD", [128, 2, 8, 128], BF16)
                    WBA = T(st, "WBA", [128, 8, 8], BF16)
                    QT = T(st, "QT", [128, SEQ], BF16)
                    KT = T(st, "KT", [128, SEQ], BF16)
                    VT = T(st, "VT", [128, SEQ], BF16)
                    XS = T(st, "XS", [128, 4 + SEQ], BF16)
                    SQ = T(st, "SQ", [128, 512], BF16)
                    RN = T(st, "RN", [128, 512])
                    DG = T(st, "DG", [128, 4, 128], BF16)
                    K_TM = T(st, "K_TM", [128, NT, 128], BF16)
                    V_TM = T(st, "V_TM", [128, NT, 128], BF16)
                    KBG = T(st, "KBG", [128, NT, 128], BF16)
                    U = T(st, "U", [128, 2, 4, 128])
                    WT = T(st, "WT", [128, 2, 4, 128], BF16)
                    ATT = T(st, "ATT", [128, 2, 4, 128], BF16)
                    ONB = T(st, "ONB", [128, 4, 128], BF16)
                    CH = T(st, "CH", [128, 2, 3, 4, 128])
                    TTb = T(st, "TTb", [128, 4, 128], BF16)
                    ES = T(st, "ES", [128, 2, 512])
                    NGL = T(st, "NGL", [128, 2, 2, 128], BF16)
                    GHL = T(st, "GHL", [128, 2, NT, 4], BF16)
                    GBt = T(st, "GBt", [128, NT, 4])
                    NGt = T(st, "NGt", [128, NT, 4])
                    BA = T(st, "BA", [128, NT, 8])
                    BETA = T(st, "BETA", [128, NT, 4])
                    LBt = T(st, "LBt", [128, NT, 4])
                    G = T(st, "G", [128, NT, 4])
                    TMP = T(st, "TMP", [128, NT, 4])
                    GTs = T(st, "GTs", [128, NT, 4])
                    EG = T(st, "EG", [128, NT, 4])
                    DL = T(st, "DL", [128, NT, 4])
                    ETL = T(st, "ETL", [128, NT, 4])
                    BEG = T(st, "BEG", [128, NT, 4])
                    RQ = T(st, "RQ", [128, NT])
                    SC1 = T(st, "SC1", [128, NT])
                    SC2 = T(st, "SC2", [128, NT])
                    VN = T(st, "VN", [128, 128], BF16)
                    BS = T(st, "BS", [128, 128])
                    OT = T(st, "OT", [128, 128])
                    OSQ = T(st, "OSQ", [128, 128])
                    SS = T(st, "SS", [128, 2])
                    SZ = XS[:, 4:4 + SEQ]

                    S.op("dve", lambda e: e.memset(XS[:, 0:4], 0.0), writes=["XS"])
                    wload(WD[:, 0], win_cols(l, C_BA - 120, 128), "WD0")
                    for t in range(NT):
                        for kc in range(8):
                            mm(PA[:, t * 8:(t + 1) * 8], HT[:, kc, t * 128:(t + 1) * 128], WD[:, 0, kc, 120:128], kc == 0, kc == 7,
                               ["WD0", f"H{kc}"], ["PA0"])
                    ckpt(10)
                    cp("act", BA[:].rearrange("p t c -> p (t c)"), PA[:, 0:128], ["PA0"], ["BA"])
                    act(BETA[:], BA[:, :, 0:4], AF.Sigmoid, ["BA"], ["BETA"])
                    act(LBt[:], BETA[:], AF.Ln, ["BETA"], ["LBt"])
                    dtb = PS[:, 164:228].rearrange("p (t h) -> p t h", h=4)
                    alog = PS[:, 100:164].rearrange("p (t h) -> p t h", h=4)
                    tt("dve", TMP[:], BA[:, :, 4:8], dtb, ALU.add, ["BA", "PSm"], ["TMP"])
                    act(TMP[:], TMP[:], AF.Exp, ["TMP"], ["TMP"])
                    act(TMP[:], TMP[:], AF.Ln, ["TMP"], ["TMP"], bias=1.0)
                    act(G[:], alog, AF.Exp, ["PSm"], ["G"])
                    stt(G[:], TMP[:], -1.0, G[:], ALU.mult, ALU.mult, ["TMP", "G"], ["G"])
                    ckpt(11)
                    cp("dve", LABC[:, 0, :, :, 0], G[:].rearrange("p t h -> p h t"), ["G"], ["LABC"])
                    cp("dve", LABC[:, 1, :, :, 0], LBt[:].rearrange("p t h -> p h t"), ["LBt"], ["LABC"])
                    ts("dve", LABC[:, 2, :, :, 1], G[:].rearrange("p t h -> p h t"), -1.0, None, ALU.mult, None, ["G"], ["LABC"])
                    ckpt(12)
                    for t in range(NT):
                        mm(PA[:, 512 + t * 4:512 + (t + 1) * 4], ltri[:], G[:, t, :], True, True, ["c_ltri", "G"], ["PA1"])
                    for t in range(NT):
                        mm(PA[:, 1024 + t * 4:1024 + (t + 1) * 4], onesf[:], G[:, t, :], True, True, ["c_onesf", "G"], ["PA2"])
                    ckpt(13)
                    flat = lambda a: a[:].rearrange("p t h -> p (t h)")
                    cp("dve", flat(GTs), PA[:, 512:576], ["PA1"], ["GTs"])
                    ckpt(14)
                    act(flat(EG), PA[:, 512:576], AF.Exp, ["PA1"], ["EG"])
                    ckpt(15)
                    act(flat(DL), PA[:, 1024:1088], AF.Exp, ["PA2"], ["DL"])
                    tt("dve", flat(ETL), PA[:, 1024:1088], flat(GTs), ALU.subtract, ["PA2", "GTs"], ["ETL"])
                    act(flat(ETL), flat(ETL), AF.Exp, ["ETL"], ["ETL"])
                    ckpt(16)
                    tt("dve", BEG[:], BETA[:], EG[:], ALU.mult, ["BETA", "EG"], ["BEG"])
                    tt("dve", GBt[:], GTs[:], LBt[:], ALU.add, ["GTs", "LBt"], ["GBt"])
                    ts("dve", NGt[:], GTs[:], -1.0, None, ALU.mult, None, ["GTs"], ["NGt"])
                    ts("dve", GHL[:, 0], G[:], -1.0, None, ALU.mult, None, ["G"], ["GHL0"])
                    tt("dve", TMP[:], G[:], GHL[:, 0], ALU.add, ["G", "GHL0"], ["TMP"])
                    ts("dve", GHL[:, 1], TMP[:], -1.0, None, ALU.mult, None, ["TMP"], ["GHL1"])
                    ckpt(17)
                    if dbg and l == 0:
                        dump("G", G[:], [128, NT, 4], F32, ["G"])
                        dump("EG", EG[:], [128, NT, 4], F32, ["EG"])
                    ckpt(1)

                    for h in range(4):
                        for sec, dst, dk_ in ((0, QT, "QT"), (1, KT, "KT"), (2, VT, "VT")):
                            ti = sec * 4 + h
                            wi = wcount[0] % 2
                            wcount[0] += 1
                            wload(WD[:, wi], win_cols(l, sec * 512 + h * 128, 128), f"WD{wi}")
                            proj_fm(WD[:, wi], f"WD{wi}", PA, "PA")
                            cp("act", XS[:, 4:4 + SEQ], PA[:], bks("PA", 0, 2048), ["XS"])
                            for k in range(4):
                                ts("dve", DG[:, k, :], identb[:], PS[:, 32 + ti * 4 + k:33 + ti * 4 + k], None, ALU.mult, None,
                                   ["identb", "PSm"], [f"DG{k}"])
                            for b in range(4):
                                for k in range(4):
                                    mm(PB[:, b * 512:(b + 1) * 512], DG[:, k, :], XS[:, 1 + b * 512 + k:1 + b * 512 + k + 512],
                                       k == 0, k == 3, [f"DG{k}", "XS"], [f"PB{b}"])
                            act(dst[:], PB[:], AF.Silu, bks("PB", 0, 2048), [dk_])
                        if dbg and l == 0 and h == 0:
                            dump("QT0", QT[:], [128, SEQ], BF16, ["QT"])
                        ckpt(2)
                        wi = wcount[0] % 2
                        wcount[0] += 1
                        wload(WD[:, wi], win_cols(l, C_Z + h * 128, 128), f"WD{wi}")
                        proj_fm(WD[:, wi], f"WD{wi}", PB, "PB")
                        act(SZ, PB[:], AF.Silu, bks("PB", 0, 2048), ["XS"])
                        for b in range(4):
                            sl = slice(b * 512, (b + 1) * 512)
                            act(SQ[:], KT[:, sl], AF.Square, ["KT"], ["SQ"])
                            mm(PA[:, sl], onesb[:], SQ[:], True, True, ["onesb", "SQ"], [f"PA{b}"])
                            rsqrt(RN[:], PA[:, sl], 1.0, 1e-6, 512, [f"PA{b}"], ["RN"])
                            tt("dve", KT[:, sl], KT[:, sl], RN[:], ALU.mult, ["KT", "RN"], ["KT"])
                        for b in range(4):
                            act(SQ[:], QT[:, b * 512:(b + 1) * 512], AF.Square, ["QT"], ["SQ"])
                            for t4 in range(4):
                                t = b * 4 + t4
                                mm(PB[:, t:t + 1], SQ[:, t4 * 128:(t4 + 1) * 128], onesb[:, 0:1], True, True, ["SQ", "onesb"], ["PB0"])
                        rsqrt(RQ[:], PB[:, 0:NT], 1.0, 1e-6, NT, ["PB0"], ["RQ"])
                        ts("dve", SC2[:], RQ[:], 128 ** -0.5, None, ALU.mult, None, ["RQ"], ["SC2"])
                        tt("dve", SC1[:], SC2[:], EG[:, :, h], ALU.mult, ["SC2", "EG"], ["SC1"])
                        for t in range(NT):
                            tr(PAb[:, t * 128:(t + 1) * 128], KT[:, t * 128:(t + 1) * 128], identb[:], ["KT", "identb"], [f"PA{t // 8}"])
                        for t in range(NT):
                            tr(PAb[:, 2048 + t * 128:2048 + (t + 1) * 128], VT[:, t * 128:(t + 1) * 128], identb[:], ["VT", "identb"], [f"PA{2 + t // 8}"])
                        cp("dve", K_TM[:].rearrange("p t d -> p (t d)"), PAb[:, 0:2048], ["PA0", "PA1"], ["K_TM"])
                        cp("act", V_TM[:].rearrange("p t d -> p (t d)"), PAb[:, 2048:4096], ["PA2", "PA3"], ["V_TM"])
                        bc = lambda a: a[:, :, h:h + 1].to_broadcast([128, NT, 128])
                        tt("dve", KBG[:], K_TM[:], bc(BEG), ALU.mult, ["K_TM", "BEG"], ["KBG"])
                        tt("dve", K_TM[:], K_TM[:], bc(ETL), ALU.mult, ["K_TM", "ETL"], ["K_TM"])
                        tt("dve", V_TM[:], V_TM[:], bc(BETA), ALU.mult, ["V_TM", "BETA"], ["V_TM"])
                        ckpt(3)
                        S.op("dve", lambda e: e.memset(Sst[:], 0.0), writes=["Sst"])
                        S.op("dve", lambda e: e.memset(Sb[:], 0.0), writes=["Sb"])
                        fl = lambda a_: a_.rearrange("p t d -> p (t d)")

                        def prescan_chain(gi, h=h):
                            ub = gi % 2
                            for tl in range(4):
                                t = gi * 4 + tl
                                bi_ = tl % 2
                                tsl = slice(t * 128, (t + 1) * 128)
                                c = slice(tl * 128, (tl + 1) * 128)
                                c1 = slice(512 + tl * 128, 512 + (tl + 1) * 128)
                                c2 = slice(1024 + tl * 128, 1024 + (tl + 1) * 128)
                                c3 = slice(1536 + tl * 128, 1536 + (tl + 1) * 128)
                                ts("dve", NGL[:, bi_, 0, :], ltrib[:], GHL[:, 0, t, h:h + 1], None, ALU.mult, None, ["ltrib", "GHL0"], [f"NGL{bi_}"])
                                ts("dve", NGL[:, bi_, 1, :], ltrib[:], GHL[:, 1, t, h:h + 1], None, ALU.mult, None, ["ltrib", "GHL1"], [f"NGL{bi_}"])
                                mm(PA[:, c], KT[:, tsl], KT[:, tsl], True, True, ["KT"], ["PA0"])
                                mm(PA[:, c1], KT[:, tsl], QT[:, tsl], True, True, ["KT", "QT"], ["PA1"])
                                for cc, mk in ((c2, 0), (c3, 3)):
                                    bk = "PA2" if mk == 0 else "PA3"
                                    mm(PA[:, cc], onesb[:], NGL[:, bi_, 0, :], True, False, ["onesb", f"NGL{bi_}"], [bk])
                                    mm(PA[:, cc], onesb[:], NGL[:, bi_, 1, :], False, False, ["onesb", f"NGL{bi_}"], [bk])
                                    mm(PA[:, cc], identb[:], nmb[:, mk, :], False, True, ["identb", "nmb"], [bk])
                                act(ES[:, 0, c], PA[:, c2], AF.Exp, ["PA2", "GBt"], ["ES0"], bias=GBt[:, t, h:h + 1])
                                act(ES[:, 1, c], PA[:, c3], AF.Exp, ["PA3", "NGt"], ["ES1"], bias=NGt[:, t, h:h + 1], scale=-1.0)
                                yield
                            tt("dve", fl(CH[:, 0, 1]), PA[:, 0:512], ES[:, 0, :], ALU.mult, ["PA0", "ES0"], ["CH01a", "CH01b"])
                            tt("dve", fl(ATT[:, ub]), PA[:, 512:1024], ES[:, 1, :], ALU.mult, ["PA1", "ES1"], [f"ATT{ub}"])
                            for tl in range(4):
                                tr(PB[:, 512 + tl * 128:512 + (tl + 1) * 128], CH[:, 0, 1, tl, :], identf[:], ["CH01a", "CH01b", "c_identf"], ["PB1"])
                            yield
                            cp("act", fl(CH[:, 0, 0]), PB[:, 512:1024], ["PB1"], ["CH00a", "CH00b"])
                            tt("dve", CH[:, 0, 2], identf[:].unsqueeze(1).to_broadcast([128, 4, 128]), CH[:, 0, 0], ALU.subtract,
                               ["c_identf", "CH00a", "CH00b"], ["CH02a", "CH02b"])
                            yield
                            halves = (("a", 0, PB, "PB"), ("b", 2, PA, "PA"))
                            cur = 0
                            for lev in range(1, 7):
                                nxt = 1 - cur
                                last = lev == 6
                                for hn, t0, PP_, pn in halves:
                                    for tl2 in range(2):
                                        tl = t0 + tl2
                                        if not last:
                                            mm(PP_[:, 512 + tl2 * 128:512 + (tl2 + 1) * 128], CH[:, cur, 1, tl, :], CH[:, cur, 0, tl, :], True, True,
                                               [f"CH{cur}1{hn}", f"CH{cur}0{hn}"], [f"{pn}1"])
                                        mm(PP_[:, 1024 + tl2 * 128:1024 + (tl2 + 1) * 128], CH[:, cur, 0, tl, :], CH[:, cur, 1, tl, :], True, True,
                                           [f"CH{cur}1{hn}", f"CH{cur}0{hn}"], [f"{pn}2"])
                                yield
                                for hn, t0, PP_, pn in halves:
                                    cp("act", fl(CH[:, nxt, 1, t0:t0 + 2, :]), PP_[:, 1024:1280], [f"{pn}2"], [f"CH{nxt}1{hn}"])
                                    if not last:
                                        cp("act", fl(CH[:, nxt, 0, t0:t0 + 2, :]), PP_[:, 512:768], [f"{pn}1"], [f"CH{nxt}0{hn}"])
                                for hn, t0, PP_, pn in halves:
                                    for tl2 in range(2):
                                        tl = t0 + tl2
                                        mm(PP_[:, 1536 + tl2 * 128:1536 + (tl2 + 1) * 128], CH[:, nxt, 1, tl, :], CH[:, cur, 2, tl, :], True, True,
                                           [f"CH{nxt}1{hn}", f"CH{cur}2{hn}"], [f"{pn}3"])
                                yield
                                for hn, t0, PP_, pn in halves:
                                    dst_ = fl(TTb[:, t0:t0 + 2, :]) if last else fl(CH[:, nxt, 2, t0:t0 + 2, :])
                                    dk_ = f"TTb{hn}" if last else f"CH{nxt}2{hn}"
                                    tt("dve", dst_, PP_[:, 1536:1792], fl(CH[:, cur, 2, t0:t0 + 2, :]), ALU.add, [f"{pn}3", f"CH{cur}2{hn}"], [dk_])
                                cur = nxt
                            for tl in range(4):
                                t = gi * 4 + tl
                                hn = "a" if tl < 2 else "b"
                                mm(PA[:, tl * 128:(tl + 1) * 128], TTb[:, tl, :], V_TM[:, t, :], True, True, [f"TTb{hn}", "V_TM"], ["PA0"])
                                mm(PA[:, 512 + tl * 128:512 + (tl + 1) * 128], KBG[:, t, :], TTb[:, tl, :], True, True, [f"TTb{hn}", "KBG"], ["PA1"])
                            yield
                            cp("act", fl(U[:, ub]), PA[:, 0:512], ["PA0"], [f"U{ub}"])
                            cp("dve", fl(WT[:, ub]), PA[:, 512:1024], ["PA1"], [f"WT{ub}"])
                            yield

                        def scan(gi, h=h):
                            ub = gi % 2
                            for tl in range(4):
                                t = gi * 4 + tl
                                tsl = slice(t * 128, (t + 1) * 128)
                                mm(PB[:, 0:128], WT[:, ub, tl, :], Sb[:], True, True, [f"WT{ub}", "Sb"], ["PB0"])
                                tt("dve", VN[:], U[:, ub, tl, :], PB[:, 0:128], ALU.subtract, [f"U{ub}", "PB0"], ["VN"])
                                mm(PB[:, 128:256], QT[:, tsl], Sb[:], True, True, ["QT", "Sb"], ["PB0"])
                                yield
                                mm(PB[:, 256:384], ATT[:, ub, tl, :], VN[:], True, True, [f"ATT{ub}", "VN"], ["PB0"])
                                mm(PB[:, 384:512], K_TM[:, t, :], VN[:], True, True, ["K_TM", "VN"], ["PB0"])
                                stt(Sst[:], Sst[:], DL[:, t, h:h + 1], PB[:, 384:512], ALU.mult, ALU.add, ["Sst", "DL", "PB0"], ["Sst"])
                                cp("act", Sb[:], Sst[:], ["Sst"], ["Sb"])
                                yield
                                act(BS[:], PB[:, 256:384], AF.Copy, ["PB0", "SC2"], ["BS"], scale=SC2[:, t:t + 1])
                                stt(OT[:], PB[:, 128:256], SC1[:, t:t + 1], BS[:], ALU.mult, ALU.add, ["PB0", "SC1", "BS"], ["OT"])
                                S.op("dve", lambda e: e.scalar_tensor_tensor(OSQ[:], OT[:], 1.0, OT[:], ALU.mult, ALU.mult, accum_out=SS[:, 0:1]),
                                     reads=["OT"], writes=["OSQ", "SS"])
                                rsqrt(SS[:, 1:2], SS[:, 0:1], 1.0 / 128, 1e-6, 1, ["SS"], ["SS1"])
                                ts("dve", ONB[:, tl, :], OT[:], SS[:, 1:2], None, ALU.mult, None, ["OT", "SS1"], ["ONB"])
                                yield
                            for tl in range(4):
                                tr(PBb[:, tl * 128:(tl + 1) * 128], ONB[:, tl, :], identb[:], ["ONB", "identb"], ["PB0"])
                            ts("dve", OB[:, h, gi * 512:(gi + 1) * 512], PBb[:, 0:512], PS[:, 80:81], None, ALU.mult, None, ["PB0", "PSm"], [f"OB{h}"])
                            yield

                        for _ in prescan_chain(0):
                            pass
                        for gi in range(NT // 4):
                            gens = [scan(gi)]
                            if gi + 1 < NT // 4:
                                gens.append(prescan_chain(gi + 1))
                            while gens:
                                for g_ in list(gens):
                                    try:
                                        next(g_)
                                    except StopIteration:
                                        gens.remove(g_)
                        tt("dve", OB[:, h, :], OB[:, h, :], SZ, ALU.mult, [f"OB{h}", "XS"], [f"OB{h}"])
                    if l == 0:
                        dump("ODN", OB[:], [128, 4, SEQ], BF16, [f"OB{h}" for h in range(4)])
                S.barrier()
                ckpt(4)

                ORET = T(mx, "ORET", [128, 4, SEQ], BF16)
                with contextlib.ExitStack() as st:
                    rc = {k: T(st, "s" + k, CONST_SHAPES[k]) for k in RET_C}
                    for k in RET_C:
                        S.dma("sp", rc[k][:], cd[k], writes=[k])
                    NH = 4
                    WR = T(st, "WR", [128, 3, 8, 512], BF16)
                    QR = T(st, "QR", [128, 2, NH, 128], BF16)
                    KR = T(st, "KR", [128, 2, NH, 128], BF16)
                    KZ = T(st, "KZ", [128, 2, NH, 128], BF16)
                    VR = T(st, "VR", [128, 2, NH, 128], BF16)
                    T1 = T(st, "T1", [128, NH, 64])
                    T2 = T(st, "T2", [128, NH, 64])
                    T3 = T(st, "T3", [128, NH, 64])
                    T4 = T(st, "T4", [128, NH, 64])
                    RQT = T(st, "RQT", [128, NH, 128], BF16)
                    RKT = T(st, "RKT", [128, NH, 128], BF16)
                    QXI = T(st, "QXI", [128, NH, 128], BF16)
                    PP = T(st, "PP", [128, NH, 128], BF16)
                    OF = T(st, "OF", [128, NH, 128])
                    OQ = T(st, "OQ", [128, NH, 128])
                    ONr = T(st, "ONr", [128, NH, 128], BF16)
                    ST = T(st, "STr", [128, 6, NH])
                    SZb = T(st, "SZb", [128, 512], BF16)
                    f2 = lambda a_: a_.rearrange("p h d -> p (h d)")
                    W_ = NH * 128
                    for i3, c0 in enumerate((C_RQ, C_RK, C_RV)):
                        wload(WR[:, i3], win_cols(l, c0, 512), f"WR{i3}")
                    S.op("dve", lambda e: e.memset(Rst[:], 0.0), writes=["Rst"])
                    S.op("dve", lambda e: e.memset(Rb[:], 0.0), writes=["Rb"])
                    def A_pe(t):
                        tsl = slice(t * 128, (t + 1) * 128)
                        for i3 in range(3):
                            for kc in range(8):
                                mm(PA[:, i3 * 512:i3 * 512 + W_], HT[:, kc, tsl], WR[:, i3, kc, :], kc == 0, kc == 7,
                                   [f"WR{i3}", f"H{kc}"], [f"PA{i3}"])

                    def A_rot(t):
                        u = t % 2
                        cosb = rc["c_cos"][:, t, :].unsqueeze(1).to_broadcast([128, NH, 64])
                        sinb = rc["c_sin"][:, t, :].unsqueeze(1).to_broadcast([128, NH, 64])
                        for i3, dst, dk_ in ((0, QR, "QR"), (1, KR, "KR")):
                            src = PA[:, i3 * 512:i3 * 512 + W_].rearrange("p (h d) -> p h d", h=NH)
                            x1, x2 = src[:, :, 0:64], src[:, :, 64:128]
                            tt("dve", T1[:], x1, cosb, ALU.mult, [f"PA{i3}", "c_cos"], ["T1"])
                            tt("dve", T2[:], x2, sinb, ALU.mult, [f"PA{i3}", "c_sin"], ["T2"])
                            tt("dve", T3[:], x2, cosb, ALU.mult, [f"PA{i3}", "c_cos"], ["T3"])
                            tt("dve", T4[:], x1, sinb, ALU.mult, [f"PA{i3}", "c_sin"], ["T4"])
                            tt("dve", dst[:, u, :, 0:64], T1[:], T2[:], ALU.subtract, ["T1", "T2"], [f"{dk_}{u}"])
                            tt("dve", dst[:, u, :, 64:128], T3[:], T4[:], ALU.add, ["T3", "T4"], [f"{dk_}{u}"])
                        cp("act", f2(VR[:, u]), PA[:, 1024:1024 + W_], ["PA2"], [f"VR{u}"])
                        tt("dve", KZ[:, u], KR[:, u], rc["c_zeta"][:], ALU.mult, [f"KR{u}", "c_zeta"], [f"KZ{u}"])

                    def B1(t):
                        u = t % 2
                        for hl in range(NH):
                            tr(PBb[:, hl * 128:(hl + 1) * 128], QR[:, u, hl, :], identb[:], [f"QR{u}", "identb"], ["PB0"])
                            tr(PBb[:, W_ + hl * 128:W_ + (hl + 1) * 128], KR[:, u, hl, :], identb[:], [f"KR{u}", "identb"], ["PB0"])
                        cp("act", f2(RQT[:]), PBb[:, 0:W_], ["PB0"], ["RQT"])
                        cp("act", f2(RKT[:]), PBb[:, W_:2 * W_], ["PB0"], ["RKT"])
                        tt("dve", QXI[:], RQT[:], rc["c_xi"][:], ALU.mult, ["RQT", "c_xi"], ["QXI"])
                        for hl in range(NH):
                            mm(PB[:, 512 + hl * 128:512 + (hl + 1) * 128], RKT[:, hl, :], RQT[:, hl, :], True, True, ["RKT", "RQT"], ["PB1"])
                        tt("dve", PP[:], PB[:, 512:512 + W_].rearrange("p (h d) -> p h d", h=NH), rc["c_rmask"][:], ALU.mult,
                           ["PB1", "c_rmask"], ["PP"])
                        for hl in range(NH):
                            mm(PB[:, 1024 + hl * 128:1024 + (hl + 1) * 128], PP[:, hl, :], VR[:, u, hl, :], True, False, ["PP", f"VR{u}"], ["PB2"])
                            mm(PB[:, 1024 + hl * 128:1024 + (hl + 1) * 128], QXI[:, hl, :], Rb[:, hl, :], False, True, ["QXI", "Rb"], ["PB2"])
                        for hl in range(NH):
                            mm(PB[:, 1536 + hl * 128:1536 + (hl + 1) * 128], KZ[:, u, hl, :], VR[:, u, hl, :], True, True, [f"KZ{u}", f"VR{u}"], ["PB3"])

                    def B2C1(t):
                        for hl in range(NH):
                            stt(Rst[:, hl, :], Rst[:, hl, :], rcd[hl], PB[:, 1536 + hl * 128:1536 + (hl + 1) * 128], ALU.mult, ALU.add,
                                ["Rst", "PB3"], ["Rst"])
                        cp("act", f2(Rb[:]), f2(Rst[:]), ["Rst"], ["Rb"])
                        cp("act", f2(OF[:]), PB[:, 1024:1024 + W_], ["PB2"], ["OF"])
                        act(OQ[:], OF[:], AF.Square, ["OF"], ["OQ"])
                        S.op("dve", lambda e: e.reduce_sum(ST[:, 0, :], OF[:], AX.X), reads=["OF"], writes=["ST0"])
                        S.op("dve", lambda e: e.reduce_sum(ST[:, 1, :], OQ[:], AX.X), reads=["OQ"], writes=["ST1"])
                        ts("dve", ST[:, 2, :], ST[:, 0, :], 1.0 / 128, None, ALU.mult, None, ["ST0"], ["ST2"])
                        tt("dve", ST[:, 3, :], ST[:, 2, :], ST[:, 2, :], ALU.mult, ["ST2"], ["ST3"])
                        stt(ST[:, 4, :], ST[:, 1, :], 1.0 / 128, ST[:, 3, :], ALU.mult, ALU.subtract, ["ST1", "ST3"], ["ST4"])
                        ts("dve", ST[:, 4, :], ST[:, 4, :], 0.0, None, ALU.max, None, ["ST4"], ["ST4"])
                        rsqrt(ST[:, 5, :], ST[:, 4, :], 1.0, 1e-5, NH, ["ST4"], ["ST5"])
                        for hl in range(NH):
                            ts("dve", ONr[:, hl, :], OF[:, hl, :], ST[:, 2, hl:hl + 1], ST[:, 5, hl:hl + 1], ALU.subtract, ALU.mult,
                               ["OF", "ST2", "ST5"], ["ONr"])

                    def C2(t):
                        tsl = slice(t * 128, (t + 1) * 128)
                        for hl in range(NH):
                            tr(PBb[:, 1024 + hl * 128:1024 + (hl + 1) * 128], ONr[:, hl, :], identb[:], ["ONr", "identb"], ["PB1"])
                        for hl in range(NH):
                            act(ORET[:, hl, tsl], PBb[:, 1024 + hl * 128:1024 + (hl + 1) * 128], AF.Copy, ["PB1", "PSm"], [f"OR{hl}"],
                                scale=PS[:, 84 + hl:85 + hl])

                    A_pe(0)
                    A_rot(0)
                    for t in range(NT):
                        B1(t)
                        if t + 1 < NT:
                            A_pe(t + 1)
                        B2C1(t)
                        if t + 1 < NT:
                            A_rot(t + 1)
                        C2(t)
                    for h in range(4):
                        wi = h % 2
                        wload(WR[:, wi, :, 0:128], win_cols(l, C_RG + h * 128, 128), f"WR{wi}")
                        proj_fm(WR[:, wi, :, 0:128], f"WR{wi}", PA, "PA")
                        for b in range(4):
                            bsl = slice(b * 512, (b + 1) * 512)
                            act(SZb[:], PA[:, bsl], AF.Silu, [f"PA{b}"], ["SZb"])
                            tt("dve", ORET[:, h, bsl], ORET[:, h, bsl], SZb[:], ALU.mult, [f"OR{h}", "SZb"], [f"OR{h}"])
                    if l == 0:
                        dump("ORET", ORET[:], [128, 4, SEQ], BF16, [f"OR{h}" for h in range(4)])
                S.barrier()

                Y = T(mx, "Y", [128, 8, SEQ], BF16)

                def merge(br, wname, SRC, skey, first=False):
                    with contextlib.ExitStack() as st:
                        WG = T(st, "WG", [128, 2, 8, 128], BF16)
                        WB = T(st, "WB", [128, 2, 4, 128], BF16)
                        SG = T(st, "SG", [128, SEQ], BF16)
                        TM = T(st, "TMg", [128, SEQ], BF16)
                        for f in range(8):
                            wi = f % 2
                            wload(WG[:, wi], win_cols(l, C_GT + br * 1024 + f * 128, 128), f"WG{wi}")
                            wload(WB[:, wi], wd[wname][l].rearrange("(k p) n -> p k n", p=128)[:, :, f * 128:(f + 1) * 128], f"WB{wi}")
                            proj_fm(WG[:, wi], f"WG{wi}", PA, "PA")
                            for kc in range(4):
                                for b in range(4):
                                    mm(PB[:, b * 512:(b + 1) * 512], WB[:, wi, kc, :], SRC[:, kc, b * 512:(b + 1) * 512], kc == 0, kc == 3,
                                       [f"WB{wi}", f"{skey}{kc}"], [f"PB{b}"])
                            act(SG[:], PA[:], AF.Sigmoid, bks("PA", 0, 2048), ["SG"])
                            if first:
                                tt("dve", Y[:, f, :], PB[:], SG[:], ALU.mult, bks("PB", 0, 2048) + ["SG"], [f"Y{f}"])
                            else:
                                tt("dve", TM[:], PB[:], SG[:], ALU.mult, bks("PB", 0, 2048) + ["SG"], ["TMg"])
                                tt("dve", Y[:, f, :], Y[:, f, :], TM[:], ALU.add, [f"Y{f}", "TMg"], [f"Y{f}"])
                    S.barrier()

                merge(0, "w_branch_dn", OB, "OB", first=True)
                ckpt(5)
                merge(2, "w_branch_ret", ORET, "OR")

                with contextlib.ExitStack() as st:
                    WD = T(st, "WDp", [128, 2, 8, 128], BF16)
                    PW = T(st, "PW", [128, 4, 128], BF16)
                    UU = T(st, "UU", [128, 16 + SEQ])
                    SA = T(st, "SA", [128, 16 + SEQ])
                    SBb = T(st, "SBb", [128, 16 + SEQ])
                    MX = T(st, "MX", [128, SEQ], BF16)
                    wload(PW[:], wd["pool_w"][l].rearrange("g c d -> c g d"), "PW")
                    for a_, k_ in ((UU, "UU"), (SA, "SA"), (SBb, "SBb")):
                        S.op("dve", lambda e: e.memset(a_[:, 0:16], 0.0), writes=[k_])
                    for gq, win in enumerate((2, 4, 8, 16)):
                        wi = gq % 2
                        wload(WD[:, wi], win_cols(l, C_PU + gq * 128, 128), f"WDp{wi}")
                        proj_fm(WD[:, wi], f"WDp{wi}", PA, "PA")
                        cp("act", UU[:, 16:], PA[:], bks("PA", 0, 2048), ["UU"])
                        src, sk = UU, "UU"
                        bufs = [(SA, "SA"), (SBb, "SBb")]
                        sh, bi = 1, 0
                        while sh < win:
                            dstb, dkk = bufs[bi]
                            tt("dve", dstb[:, 16:], src[:, 16:], src[:, 16 - sh:16 + SEQ - sh], ALU.add, [sk], [dkk])
                            src, sk = dstb, dkk
                            bi = 1 - bi
                            sh *= 2
                        stt(MX[:], src[:, 16:], 1.0 / win, UU[:, 16:], ALU.mult, ALU.subtract, [sk, "UU"], ["MX"])
                        dstb, dkk = bufs[bi]
                        tt("dve", dstb[:, 16:32], src[:, 16:32], cs["c_pinv"][:, gq, :], ALU.mult, [sk, "c_pinv"], [dkk])
                        tt("dve", MX[:, 0:16], dstb[:, 16:32], UU[:, 16:32], ALU.subtract, [dkk, "UU"], ["MX"])
                        for b in range(4):
                            mm(PB[:, b * 512:(b + 1) * 512], PW[:, gq, :], MX[:, b * 512:(b + 1) * 512], True, True, ["PW", "MX"], [f"PB{b}"])
                        ts("dve", OB[:, gq, :], PB[:], PS[:, 96 + gq:97 + gq], None, ALU.mult, None, bks("PB", 0, 2048) + ["PSm"], [f"OB{gq}"])
                    if l == 0:
                        dump("OPOOL", OB[:], [128, 4, SEQ], BF16, [f"OB{h}" for h in range(4)])
                S.barrier()
                merge(1, "w_branch_pool", OB, "OB")
                ckpt(6)

                ckpt(7)
                if l == 0:
                    dump("Y", Y[:], [128, 8, SEQ], BF16, [f"Y{f}" for f in range(8)])

                with contextlib.ExitStack() as st:
                    BW = 512
                    WO = OB[:].rearrange("p h s -> p (h s)").rearrange("p (k n) -> p k n", k=8)
                    M2 = HT[:].rearrange("p f s -> p (f s)").bitcast(F32).rearrange("p (u f s) -> p u f s", u=2, f=8)
                    SQ = T(st, "SQo", [128, 2, BW], BF16)
                    RS = T(st, "RSo", [128, BW])
                    TMo = T(st, "TMo", [128, BW])
                    wload(WO[:, 0:4, :], wd["w_out"][l].rearrange("(k p) n -> p k n", p=128)[:, 0:4, :], "WOa")
                    wload(WO[:, 4:8, :], wd["w_out"][l].rearrange("(k p) n -> p k n", p=128)[:, 4:8, :], "WOb")
                    for b in range(SEQ // BW):
                        bsl = slice(b * BW, (b + 1) * BW)
                        M = M2[:, b % 2]
                        mk_ = f"M{b % 2}_"
                        for f in range(8):
                            pr = PA[:, (f % 4) * 512:(f % 4) * 512 + BW]
                            for kc in range(8):
                                mm(pr, WO[:, kc, f * 128:(f + 1) * 128], Y[:, kc, bsl], kc == 0, kc == 7,
                                   ["WOa" if kc < 4 else "WOb", f"Y{kc}"], [f"PA{f % 4}"])
                            cp("act", M[:, f, :], pr, [f"PA{f % 4}"], [mk_ + str(f)])
                            act(SQ[:, f % 2, :], M[:, f, :], AF.Square, [mk_ + str(f)], [f"SQo{f % 2}"])
                            mm(PB[:, 0:BW], onesb[:], SQ[:, f % 2, :], f == 0, f == 7, ["onesb", f"SQo{f % 2}"], ["PB0"])
                        rsqrt(RS[:], PB[:, 0:BW], 1.0 / D, 1e-6, BW, ["PB0"], ["RSo"])
                        for f in range(8):
                            stt(TMo[:], M[:, f, :], PS[:, 8 + f:9 + f], RS[:], ALU.mult, ALU.mult, [mk_ + str(f), "RSo", "PSm"], ["TMo"])
                            tt("dve", X[:, f, bsl], X[:, f, bsl], TMo[:], ALU.add, [f"X{f}", "TMo"], [f"X{f}"])
                S.barrier()
            S.barrier()
            if l == 0:
                dump("XMIX", X[:], [128, 8, SEQ], F32, [f"X{f}" for f in range(8)])
            ckpt(8)

            with contextlib.ExitStack() as st:
                HTh = T(st, "HTh", [128, 8, 1024], BF16)
                A = T(st, "A", [128, NJ, 1024], BF16)
                M = T(st, "Mf", [128, 8, 1024])
                WG = T(st, "WGf", [128, 2, 8, 256], BF16)
                WU = T(st, "WUf", [128, 2, 8, 256], BF16)
                WDn = T(st, "WDn", [128, 2, NJ, 128], BF16)
                SGt = T(st, "SGt", [128, 1024], BF16)
                SQ = T(st, "SQf", [128, 2, 512], BF16)
                RS = T(st, "RSf", [128, 2, 512])
                TMf = T(st, "TMf", [128, 512])

                def gen_pre(hf):
                    for bi, blk in enumerate((2 * hf, 2 * hf + 1)):
                        bsl = slice(blk * 512, (blk + 1) * 512)
                        for f in range(8):
                            act(SQ[:, f % 2, :], X[:, f, bsl], AF.Square, [f"X{f}"], [f"SQf{f % 2}"])
                            mm(PB[:, 1536:2048], onesb[:], SQ[:, f % 2, :], f == 0, f == 7, ["onesb", f"SQf{f % 2}"], ["PB3"])
                            if f % 2 == 1:
                                yield
                        rsqrt(RS[:, 0, :], PB[:, 1536:2048], 1.0 / D, 1e-6, 512, ["PB3"], ["RSf0"])
                        yield
                        for f in range(8):
                            stt(HTh[:, f, bi * 512:(bi + 1) * 512], X[:, f, bsl], PS[:, 16 + f:17 + f], RS[:, 0, :], ALU.mult, ALU.mult,
                                [f"X{f}", "RSf0", "PSm"], [f"Hh{f}"])
                            if f % 2 == 1:
                                yield

                def gate_up(hf, side=None):
                    for jc in range(11):
                        wi = jc % 2
                        wload(WG[:, wi], wd["ffn_gate"][l].rearrange("(k p) n -> p k n", p=128)[:, :, jc * 256:jc * 256 + 256], f"WGf{wi}")
                        wload(WU[:, wi], wd["ffn_up"][l].rearrange("(k p) n -> p k n", p=128)[:, :, jc * 256:jc * 256 + 256], f"WUf{wi}")
                        for jj in range(2):
                            j = jc * 2 + jj
                            po = (j % 2) * 1024
                            for kc in range(8):
                                for b2 in range(2):
                                    mm(PA[:, po + b2 * 512:po + (b2 + 1) * 512], WG[:, wi, kc, jj * 128:(jj + 1) * 128], HTh[:, kc, b2 * 512:(b2 + 1) * 512],
                                       kc == 0, kc == 7, [f"WGf{wi}", f"Hh{kc}"], [f"PA{(po // 512) + b2}"])
                            for kc in range(8):
                                for b2 in range(2):
                                    mm(PB[:, po + b2 * 512:po + (b2 + 1) * 512], WU[:, wi, kc, jj * 128:(jj + 1) * 128], HTh[:, kc, b2 * 512:(b2 + 1) * 512],
                                       kc == 0, kc == 7, [f"WUf{wi}", f"Hh{kc}"], [f"PB{(po // 512) + b2}"])
                            act(SGt[:], PA[:, po:po + 1024], AF.Silu, bks("PA", po, po + 1024), ["SGt"])
                            tt("dve", A[:, j, :], PB[:, po:po + 1024], SGt[:], ALU.mult, bks("PB", po, po + 1024) + ["SGt"], [f"A{j}"])
                            if side is not None:
                                for _ in range(2):
                                    next(side, None)

                def down(hf, side=None):
                    for f in range(8):
                        wi = f % 2
                        wload(WDn[:, wi], wd["ffn_down"][l].rearrange("(j p) n -> p j n", p=128)[:, :, f * 128:(f + 1) * 128], f"WDn{wi}")
                        po = (f % 2) * 1024
                        for j in range(NJ):
                            for b2 in range(2):
                                mm(PA[:, po + b2 * 512:po + (b2 + 1) * 512], WDn[:, wi, j, :], A[:, j, b2 * 512:(b2 + 1) * 512], j == 0, j == NJ - 1,
                                   [f"WDn{wi}", f"A{j}"], [f"PA{(po // 512) + b2}"])
                        if side is not None:
                            for _ in range(3):
                                next(side, None)
                        cp("act", M[:, f, :], PA[:, po:po + 1024], bks("PA", po, po + 1024), [f"Mf{f}"])
                        for b2 in range(2):
                            act(SQ[:, b2, :], M[:, f, b2 * 512:(b2 + 1) * 512], AF.Square, [f"Mf{f}"], [f"SQf{b2}"])
                            mm(PB[:, b2 * 512:(b2 + 1) * 512], onesb[:], SQ[:, b2, :], f == 0, f == 7, ["onesb", f"SQf{b2}"], [f"PB{b2}"])
                    if side is not None:
                        for _ in side:
                            pass
                    for b2 in range(2):
                        rsqrt(RS[:, b2, :], PB[:, b2 * 512:(b2 + 1) * 512], 1.0 / D, 1e-6, 512, [f"PB{b2}"], [f"RSf{b2}"])

                def gen_post(hf):
                    for b2 in range(2):
                        hsl = slice(hf * 1024 + b2 * 512, hf * 1024 + (b2 + 1) * 512)
                        for f in range(8):
                            stt(TMf[:], M[:, f, b2 * 512:(b2 + 1) * 512], PS[:, 24 + f:25 + f], RS[:, b2, :], ALU.mult, ALU.mult,
                                [f"Mf{f}", f"RSf{b2}", "PSm"], ["TMf"])
                            tt("dve", X[:, f, hsl], X[:, f, hsl], TMf[:], ALU.add, [f"X{f}", "TMf"], [f"X{f}"])
                            yield

                for _ in gen_pre(0):
                    pass
                gate_up(0)
                down(0, side=gen_pre(1))
                gp = gen_post(0)
                gate_up(1, side=gp)
                for _ in gp:
                    pass
                down(1)
                for _ in gen_post(1):
                    pass
            S.barrier()

        except _Stop:
            pass
        S.off = False
        for f in range(8):
            S.dma("sp", yT[f * 128:(f + 1) * 128, :], X[:, f, :], reads=[f"X{f}"])
        S.barrier()
    return nc, dbg_outs


def pack_small(inp, l):
    ps = np.zeros((128, NSM), np.float32)
    fm = lambda v: np.asarray(v, np.float32).reshape(-1, 128).T
    ps[:, 0:8] = fm(inp["mix_pre_norm"][l])
    ps[:, 8:16] = fm(inp["mix_post_norm"][l])
    ps[:, 16:24] = fm(inp["ffn_pre_norm"][l])
    ps[:, 24:32] = fm(inp["ffn_post_norm"][l])
    cv = np.asarray(inp["dn_conv"][l], np.float32)
    ps[:, 32:80] = cv.reshape(4, 12, 128).transpose(2, 1, 0).reshape(128, 48)
    ps[:, 80] = np.asarray(inp["dn_out_norm"][l], np.float32)
    ps[:, 84:88] = fm(inp["ret_out_norm"][l])
    ps[:, 96:100] = fm(inp["pool_scale"][l])
    ps[:, 100:164] = np.tile(np.asarray(inp["dn_A_log"][l], np.float32)[None, :], (128, NT))
    ps[:, 164:228] = np.tile(np.asarray(inp["dn_dt_bias"][l], np.float32)[None, :], (128, NT))
    return ps


_CACHE = {}


def run_layers(inp, x_fm_list, layers, dbg=False):
    consts = host_consts()
    rcd = consts.pop("_rcd")
    NL = len(layers)
    key = (NL, dbg)
    if key not in _CACHE:
        _CACHE[key] = build_program(NL, rcd, dbg)
    nc, dbg_outs = _CACHE[key]
    shared = {k: v for k, v in consts.items()}
    shared["p_small"] = np.stack([pack_small(inp, l) for l in layers])
    for k in W_SHAPES:
        shared[k] = np.ascontiguousarray(np.asarray(inp[k], np.float32)[layers[0]:layers[-1] + 1])
    in_maps = [dict(shared, xT=np.ascontiguousarray(x)) for x in x_fm_list]
    res = run_bass_kernel_spmd(nc, in_maps, core_ids=list(range(len(x_fm_list))))
    return res


FUSED = True


def kernel(**inputs):
    x = np.asarray(inputs["x"], np.float32)
    xs = [np.ascontiguousarray(x[b].T) for b in range(x.shape[0])]
    if FUSED:
        res = run_layers(inputs, xs, [0, 1, 2, 3])
        xs = [r["yT"] for r in res.results]
    else:
        for l in range(4):
            res = run_layers(inputs, xs, [l])
            xs = [np.asarray(r["yT"], np.float32) for r in res.results]
    return np.stack([np.asarray(y, np.float32).T for y in xs]).astype(np.float32)
```
